# Optimizing a Trainium2 kernel written in Bass

```python
import math
import jax, jax.numpy as jnp
from jax import lax
import numpy as np

D_MODEL = 1024
BATCH = 2
SEQ = 8192
DEPTH = 4

D_MIX = D_MODEL
POOL_WIDTH = 256
POOL_WINDOWS = (2, 4, 8, 16)
POOL_GROUP = POOL_WIDTH // len(POOL_WINDOWS)
NSA_HEADS = 8
HEAD_DIM = 64
NSA_WIDTH = NSA_HEADS * HEAD_DIM
NSA_KV_HEADS = 2
NSA_REP = NSA_HEADS // NSA_KV_HEADS
KV_WIDTH = NSA_KV_HEADS * HEAD_DIM
CMP_BLOCK = 32
CMP_STRIDE = 16
CMP_HIDDEN = 128
SLC_BLOCK = 64
SLC_TOP_N = 16
WINDOW = 512
Q_BLOCK = 128
ROPE_THETA = 10000.0
FORCE_BONUS = 1e4
NEG_INF = -1e30
SSM_WIDTH = 256
SSM_GROUP = 16
SSM_GROUPS = SSM_WIDTH // SSM_GROUP
SSM_STATE = 64
DT_MIN = 1e-3
DT_MAX = 1e-1
D_FF = 2816
LN_EPS = 1e-5
ALPHA = (2.0 * DEPTH) ** 0.25
BETA = (8.0 * DEPTH) ** -0.25
IN_COLS = POOL_WIDTH + NSA_WIDTH + 6 * KV_WIDTH + 3 * NSA_HEADS + SSM_WIDTH
SPLITS = (POOL_WIDTH,
          POOL_WIDTH + NSA_WIDTH,
          POOL_WIDTH + NSA_WIDTH + 6 * KV_WIDTH,
          POOL_WIDTH + NSA_WIDTH + 6 * KV_WIDTH + 3 * NSA_HEADS)

kernel_name = 'hybrid_pool_nsa_s5_macaron_deepnorm'


def layer_norm(x, g, b):
    xf = x.astype(jnp.float32)
    mu = jnp.mean(xf, axis=-1, keepdims=True)
    var = jnp.mean(jnp.square(xf - mu), axis=-1, keepdims=True)
    return ((xf - mu) * lax.rsqrt(var + LN_EPS) * g + b).astype(x.dtype)


def swiglu(x, w_in, w_out):
    gate, up = jnp.split(x @ w_in, 2, axis=-1)
    return (jax.nn.silu(gate) * up) @ w_out


def rope_tables(seq):
    inv = 1.0 / (ROPE_THETA ** (jnp.arange(0, HEAD_DIM, 2, dtype=jnp.float32) / HEAD_DIM))
    ang = jnp.arange(seq, dtype=jnp.float32)[:, None] * inv[None, :]
    return jnp.cos(ang), jnp.sin(ang)


def apply_rope(x, cos, sin):
    x1, x2 = jnp.split(x, 2, axis=-1)
    c = cos.astype(x.dtype)
    s = sin.astype(x.dtype)
    return jnp.concatenate([x1 * c - x2 * s, x2 * c + x1 * s], axis=-1)


def pool_mixer(u, w_pool, scale):
    bsz, seq, _ = u.shape
    uf = u.astype(jnp.float32)
    maxw = POOL_WINDOWS[-1]
    csum = jnp.pad(jnp.cumsum(uf, axis=1), ((0, 0), (maxw, 0), (0, 0)))
    t = jnp.arange(seq, dtype=jnp.float32)[None, :, None]
    outs = []
    for gidx, w in enumerate(POOL_WINDOWS):
        sl = slice(gidx * POOL_GROUP, (gidx + 1) * POOL_GROUP)
        tot = csum[:, maxw:, sl] - csum[:, maxw - w:maxw - w + seq, sl]
        outs.append(tot / jnp.minimum(t + 1.0, float(w)) - uf[..., sl])
    pooled = jnp.stack(outs, axis=2).astype(u.dtype)
    mixed = jnp.einsum('bsgc,gcd->bsgd', pooled, w_pool)
    return mixed.reshape(bsz, seq, POOL_WIDTH) * scale


def nsa_mixer(q, k_cmp, v_cmp, k_slc, v_slc, k_win, v_win, gates,
              pe_k, pe_v, ck_w1, ck_w2, cv_w1, cv_w2):
    bsz, _, seq, _ = q.shape
    n_cmp = (seq - CMP_BLOCK) // CMP_STRIDE + 1
    n_slc = seq // SLC_BLOCK
    n_sel = min(SLC_TOP_N, n_slc)
    n_qb = seq // Q_BLOCK
    scale = HEAD_DIM ** -0.5

    cidx = jnp.arange(n_cmp)[:, None] * CMP_STRIDE + jnp.arange(CMP_BLOCK)[None, :]
    cmp_start = cidx[:, 0]
    cmp_end = cidx[:, -1]

    def compress(k, pe, w1, w2):
        kb = (k[:, :, cidx] + pe).reshape(bsz, NSA_KV_HEADS, n_cmp, CMP_BLOCK * HEAD_DIM)
        return jax.nn.gelu(kb @ w1) @ w2

    kc = compress(k_cmp, pe_k, ck_w1, ck_w2)
    vc = compress(v_cmp, pe_v, cv_w1, cv_w2)

    slc_start = jnp.arange(n_slc) * SLC_BLOCK
    overlap = ((cmp_start[:, None] < slc_start[None, :] + SLC_BLOCK)
               & (cmp_end[:, None] >= slc_start[None, :])).astype(jnp.float32)

    ks_blocks = k_slc.reshape(bsz, NSA_KV_HEADS, n_slc, SLC_BLOCK, HEAD_DIM)
    vs_blocks = v_slc.reshape(bsz, NSA_KV_HEADS, n_slc, SLC_BLOCK, HEAD_DIM)
    pad = ((0, 0), (0, 0), (WINDOW, 0), (0, 0))
    kw_pad = jnp.pad(k_win, pad)
    vw_pad = jnp.pad(v_win, pad)
    bi = jnp.arange(bsz)[:, None, None, None]
    gi = jnp.arange(NSA_KV_HEADS)[None, :, None, None]
    blk_ids = jnp.arange(n_slc)
    in_blk = jnp.arange(SLC_BLOCK)
    win_off = jnp.arange(WINDOW + Q_BLOCK)

    def masked_softmax(s, mask):
        s = jnp.where(mask, s.astype(jnp.float32) * scale, NEG_INF)
        return jnp.where(mask, jax.nn.softmax(s, axis=-1), 0.0)

    def one_block(args):
        qb, gb, q0 = args
        tq = q0 + jnp.arange(Q_BLOCK)
        valid_c = cmp_end[None, :] <= tq[:, None]
        p_c = masked_softmax(jnp.einsum('bgrqd,bgcd->bgrqc', qb, kc), valid_c)
        o_cmp = jnp.einsum('bgrqc,bgcd->bgrqd', p_c.astype(vc.dtype), vc)
        imp = jnp.einsum('bgrqc,cj->bgqj', p_c, overlap)
        cur = tq // SLC_BLOCK
        forced = ((blk_ids[None, :] == 0) | (blk_ids[None, :] == cur[:, None])
                  | (blk_ids[None, :] == cur[:, None] - 1))
        imp = jnp.where(forced, imp + FORCE_BONUS, imp)
        imp = jnp.where(slc_start[None, :] <= tq[:, None], imp, NEG_INF)
        _, sel = lax.top_k(imp, n_sel)
        ks = ks_blocks[bi, gi, sel].reshape(bsz, NSA_KV_HEADS, Q_BLOCK, n_sel * SLC_BLOCK, HEAD_DIM)
        vs = vs_blocks[bi, gi, sel].reshape(bsz, NSA_KV_HEADS, Q_BLOCK, n_sel * SLC_BLOCK, HEAD_DIM)
        kpos = (sel[..., None] * SLC_BLOCK + in_blk).reshape(bsz, NSA_KV_HEADS, Q_BLOCK, n_sel * SLC_BLOCK)
        valid_s = (kpos <= tq[None, None, :, None])[:, :, None]
        p_s = masked_softmax(jnp.einsum('bgrqd,bgqkd->bgrqk', qb, ks), valid_s)
        o_slc = jnp.einsum('bgrqk,bgqkd->bgrqd', p_s.astype(vs.dtype), vs)
        kw = lax.dynamic_slice_in_dim(kw_pad, q0, WINDOW + Q_BLOCK, axis=2)
        vw = lax.dynamic_slice_in_dim(vw_pad, q0, WINDOW + Q_BLOCK, axis=2)
        kwpos = q0 - WINDOW + win_off
        dist = tq[:, None] - kwpos[None, :]
        valid_w = (dist >= 0) & (dist < WINDOW) & (kwpos[None, :] >= 0)
        p_w = masked_softmax(jnp.einsum('bgrqd,bgkd->bgrqk', qb, kw), valid_w)
        o_win = jnp.einsum('bgrqk,bgkd->bgrqd', p_w.astype(vw.dtype), vw)
        return gb[..., 0:1] * o_cmp + gb[..., 1:2] * o_slc + gb[..., 2:3] * o_win

    q_blocks = jnp.moveaxis(q.reshape(bsz, NSA_KV_HEADS, NSA_REP, n_qb, Q_BLOCK, HEAD_DIM), 3, 0)
    g_blocks = jnp.moveaxis(gates.reshape(bsz, NSA_KV_HEADS, NSA_REP, n_qb, Q_BLOCK, 3), 3, 0)
    starts = jnp.arange(n_qb, dtype=jnp.int32) * Q_BLOCK
    out = lax.map(one_block, (q_blocks, g_blocks, starts))
    out = jnp.moveaxis(out, 0, 3).reshape(bsz, NSA_HEADS, seq, HEAD_DIM)
    return out.transpose(0, 2, 1, 3).reshape(bsz, seq, NSA_WIDTH)


def s5_mixer(u, lam_re, lam_im, log_dt, b_re, b_im, c_re, c_im, d_skip, glu_w, glu_b):
    bsz, seq, _ = u.shape
    uf = u.astype(jnp.float32).reshape(bsz, seq, SSM_GROUPS, SSM_GROUP)
    dt = jnp.exp(log_dt.astype(jnp.float32))[:, None]
    lr = lam_re.astype(jnp.float32)
    li = lam_im.astype(jnp.float32)
    mag = jnp.exp(lr * dt)
    ar = mag * jnp.cos(li * dt)
    ai = mag * jnp.sin(li * dt)
    den = lr * lr + li * li
    fr = ((ar - 1.0) * lr + ai * li) / den
    fi = (ai * lr - (ar - 1.0) * li) / den
    br = b_re.astype(jnp.float32)
    bim = b_im.astype(jnp.float32)
    bbr = fr[..., None] * br - fi[..., None] * bim
    bbi = fr[..., None] * bim + fi[..., None] * br
    xr = jnp.einsum('bsgc,gpc->bsgp', uf, bbr)
    xi = jnp.einsum('bsgc,gpc->bsgp', uf, bbi)
    a_r = jnp.broadcast_to(ar, xr.shape)
    a_i = jnp.broadcast_to(ai, xr.shape)

    def combine(e1, e2):
        a1r, a1i, b1r, b1i = e1
        a2r, a2i, b2r, b2i = e2
        return (a2r * a1r - a2i * a1i, a2r * a1i + a2i * a1r,
                a2r * b1r - a2i * b1i + b2r, a2r * b1i + a2i * b1r + b2i)

    _, _, hr, hi = lax.associative_scan(combine, (a_r, a_i, xr, xi), axis=1)
    y = (jnp.einsum('bsgp,gcp->bsgc', hr, c_re.astype(jnp.float32))
         - jnp.einsum('bsgp,gcp->bsgc', hi, c_im.astype(jnp.float32))
         + d_skip.astype(jnp.float32) * uf)
    y = jax.nn.gelu(y.reshape(bsz, seq, SSM_WIDTH)).astype(u.dtype)
    return y * jax.nn.sigmoid(y @ glu_w + glu_b)


def hybrid_mixer(h, cos, sin, w_in, w_out, pool_w, pool_scale,
                 pe_k, pe_v, ck_w1, ck_w2, cv_w1, cv_w2,
                 lam_re, lam_im, log_dt, b_re, b_im, c_re, c_im, d_skip, glu_w, glu_b):
    bsz, seq, _ = h.shape
    z = h @ w_in
    u_pool, q, kv, g, u_ssm = jnp.split(z, SPLITS, axis=-1)
    q = apply_rope(q.reshape(bsz, seq, NSA_HEADS, HEAD_DIM).transpose(0, 2, 1, 3), cos, sin)
    kv = kv.reshape(bsz, seq, 6, NSA_KV_HEADS, HEAD_DIM).transpose(2, 0, 3, 1, 4)
    k_cmp = apply_rope(kv[0], cos, sin)
    k_slc = apply_rope(kv[2], cos, sin)
    k_win = apply_rope(kv[4], cos, sin)
    gates = jax.nn.sigmoid(g).reshape(bsz, seq, NSA_HEADS, 3).transpose(0, 2, 1, 3)
    o_pool = pool_mixer(u_pool, pool_w, pool_scale)
    o_nsa = nsa_mixer(q, k_cmp, kv[1], k_slc, kv[3], k_win, kv[5], gates,
                      pe_k, pe_v, ck_w1, ck_w2, cv_w1, cv_w2)
    o_ssm = s5_mixer(u_ssm, lam_re, lam_im, log_dt, b_re, b_im, c_re, c_im, d_skip, glu_w, glu_b)
    return jnp.concatenate([o_pool, o_nsa.astype(o_pool.dtype), o_ssm.astype(o_pool.dtype)], axis=-1) @ w_out


def setup_inputs(seed: int = 0) -> dict:
    key = jax.random.key(seed)
    ks = jax.random.split(key, 32)
    nrm = lambda k, shape, s: jax.random.normal(k, shape, jnp.float32) * s
    L = DEPTH
    n_idx = jnp.arange(SSM_STATE, dtype=jnp.float32)
    log_dt = (math.log(DT_MIN) + jax.random.uniform(ks[20], (L, SSM_GROUPS), jnp.float32)
              * (math.log(DT_MAX) - math.log(DT_MIN)))
    return {
        'x': nrm(ks[0], (BATCH, SEQ, D_MODEL), 1.0),
        'ln_g': 1.0 + nrm(ks[1], (L, 3, D_MODEL), 0.05),
        'ln_b': nrm(ks[2], (L, 3, D_MODEL), 0.02),
        'ffn1_in': nrm(ks[3], (L, D_MODEL, 2 * D_FF), D_MODEL ** -0.5),
        'ffn1_out': nrm(ks[4], (L, D_FF, D_MODEL), BETA * D_FF ** -0.5),
        'ffn2_in': nrm(ks[5], (L, D_MODEL, 2 * D_FF), D_MODEL ** -0.5),
        'ffn2_out': nrm(ks[6], (L, D_FF, D_MODEL), BETA * D_FF ** -0.5),
        'w_in': nrm(ks[7], (L, D_MODEL, IN_COLS), D_MODEL ** -0.5),
        'w_out': nrm(ks[8], (L, D_MIX, D_MODEL), BETA * D_MIX ** -0.5),
        'pool_w': nrm(ks[9], (L, len(POOL_WINDOWS), POOL_GROUP, POOL_GROUP), POOL_GROUP ** -0.5),
        'pool_scale': 1.0 + nrm(ks[10], (L, POOL_WIDTH), 0.1),
        'cmp_pe_k': nrm(ks[11], (L, CMP_BLOCK, HEAD_DIM), 0.1),
        'cmp_pe_v': nrm(ks[12], (L, CMP_BLOCK, HEAD_DIM), 0.1),
        'cmp_k_w1': nrm(ks[13], (L, CMP_BLOCK * HEAD_DIM, CMP_HIDDEN), (CMP_BLOCK * HEAD_DIM) ** -0.5),
        'cmp_k_w2': nrm(ks[14], (L, CMP_HIDDEN, HEAD_DIM), CMP_HIDDEN ** -0.5),
        'cmp_v_w1': nrm(ks[15], (L, CMP_BLOCK * HEAD_DIM, CMP_HIDDEN), (CMP_BLOCK * HEAD_DIM) ** -0.5),
        'cmp_v_w2': nrm(ks[16], (L, CMP_HIDDEN, HEAD_DIM), CMP_HIDDEN ** -0.5),
        'ssm_lam_re': -0.5 + nrm(ks[17], (L, SSM_GROUPS, SSM_STATE), 0.01),
        'ssm_lam_im': math.pi * n_idx + nrm(ks[18], (L, SSM_GROUPS, SSM_STATE), 0.01),
        'ssm_log_dt': log_dt,
        'ssm_b_re': nrm(ks[21], (L, SSM_GROUPS, SSM_STATE, SSM_GROUP), (2.0 * SSM_GROUP) ** -0.5),
        'ssm_b_im': nrm(ks[22], (L, SSM_GROUPS, SSM_STATE, SSM_GROUP), (2.0 * SSM_GROUP) ** -0.5),
        'ssm_c_re': nrm(ks[23], (L, SSM_GROUPS, SSM_GROUP, SSM_STATE), SSM_STATE ** -0.5),
        'ssm_c_im': nrm(ks[24], (L, SSM_GROUPS, SSM_GROUP, SSM_STATE), SSM_STATE ** -0.5),
        'ssm_d': nrm(ks[25], (L, SSM_GROUPS, SSM_GROUP), 1.0),
        'ssm_glu_w': nrm(ks[26], (L, SSM_WIDTH, SSM_WIDTH), SSM_WIDTH ** -0.5),
        'ssm_glu_b': nrm(ks[27], (L, SSM_WIDTH), 0.02),
    }


def reference(x, ln_g, ln_b, ffn1_in, ffn1_out, ffn2_in, ffn2_out, w_in, w_out,
              pool_w, pool_scale, cmp_pe_k, cmp_pe_v, cmp_k_w1, cmp_k_w2, cmp_v_w1, cmp_v_w2,
              ssm_lam_re, ssm_lam_im, ssm_log_dt, ssm_b_re, ssm_b_im, ssm_c_re, ssm_c_im,
              ssm_d, ssm_glu_w, ssm_glu_b):
    cos, sin = rope_tables(x.shape[1])
    h = x
    for l in range(DEPTH):
        h = layer_norm(ALPHA * h + 0.5 * swiglu(h, ffn1_in[l], ffn1_out[l]), ln_g[l, 0], ln_b[l, 0])
        mix = hybrid_mixer(h, cos, sin, w_in[l], w_out[l], pool_w[l], pool_scale[l],
                           cmp_pe_k[l], cmp_pe_v[l], cmp_k_w1[l], cmp_k_w2[l], cmp_v_w1[l], cmp_v_w2[l],
                           ssm_lam_re[l], ssm_lam_im[l], ssm_log_dt[l], ssm_b_re[l], ssm_b_im[l],
                           ssm_c_re[l], ssm_c_im[l], ssm_d[l], ssm_glu_w[l], ssm_glu_b[l])
        h = layer_norm(ALPHA * h + mix, ln_g[l, 1], ln_b[l, 1])
        h = layer_norm(ALPHA * h + 0.5 * swiglu(h, ffn2_in[l], ffn2_out[l]), ln_g[l, 2], ln_b[l, 2])
    return h
```

```python
from contextlib import ExitStack
import numpy as np
import numpy as np
import concourse.bass as bass
import concourse.mybir as mybir

F32 = mybir.dt.float32
BF16 = mybir.dt.bfloat16
AF = mybir.ActivationFunctionType
ALU = mybir.AluOpType
AX = mybir.AxisListType


class Buf:
    __slots__ = ("name", "w", "r")

    def __init__(self, name=""):
        self.name = name
        self.w = None
        self.r = {}


class TT:
    def __init__(self, t, name):
        self.t = t
        self.b = Buf(name)

    def __getitem__(self, k):
        return self.t[k]


class Prog:
    CE = ("pe", "dve", "act", "pool")
    ENG = ("pe", "dve", "act", "pool", "sp")

    def __init__(self, nc, es, n_dma_sems=24):
        self.nc = nc
        self.es = es
        self.ops = {e: [] for e in self.ENG}
        self.sem = {e: es.enter_context(nc.semaphore("s_" + e)) for e in self.CE}
        self.cnt = {e: 0 for e in self.CE}
        self.dsem = [es.enter_context(nc.semaphore("d%d" % i)) for i in range(n_dma_sems)]
        self.dcnt = [0] * n_dma_sems
        self.dnext = 0
        self.waited = {e: {} for e in self.ENG}
        self.nops = 0

    def sb(self, name, shape, dtype=F32, es=None):
        self.uid = getattr(self, "uid", 0) + 1
        t = (es or self.es).enter_context(self.nc.sbuf_tensor("sb%d_%s" % (self.uid, name), list(shape), dtype))
        return TT(t, name)

    def ps(self, name, shape, dtype=F32, es=None):
        self.uid = getattr(self, "uid", 0) + 1
        t = (es or self.es).enter_context(self.nc.psum_tensor("ps%d_%s" % (self.uid, name), list(shape), dtype))
        return TT(t, name)

    def _semh(self, key):
        if isinstance(key, tuple):
            return self.dsem[key[1]]
        return self.sem[key]

    def _deps(self, eng, reads, writes, extra=()):
        need = {}

        def add(src):
            if src is None:
                return
            key, val = src
            if need.get(key, 0) < val:
                need[key] = val

        for b in reads:
            add(b.w)
        for b in writes:
            add(b.w)
            for k, v in b.r.items():
                add((k, v))
        for s in extra:
            add(s)
        waits = []
        for key, val in need.items():
            if key == eng and eng == "pe":
                continue
            if self.waited[eng].get(key, 0) >= val:
                continue
            self.waited[eng][key] = val
            waits.append((self._semh(key), val))
        return waits

    def _mark(self, src, reads, writes):
        key, val = src
        for b in reads:
            if b.r.get(key, 0) < val:
                b.r[key] = val
        for b in writes:
            b.w = src
            b.r = {}

    @staticmethod
    def _bufs(xs):
        out = []
        for x in xs:
            if x is None:
                continue
            out.append(x.b if isinstance(x, TT) else x)
        return out

    def op(self, eng, fn, reads=(), writes=()):
        reads = self._bufs(reads)
        writes = self._bufs(writes)
        waits = self._deps(eng, reads, writes)
        self.cnt[eng] += 1
        src = (eng, self.cnt[eng])
        self.ops[eng].append((fn, waits, (self.sem[eng], 1)))
        self._mark(src, reads, writes)
        self.nops += 1
        return src

    def dma(self, out, in_, reads=(), writes=(), q="sp", **kw):
        reads = self._bufs(reads)
        writes = self._bufs(writes)
        k = self.dnext % len(self.dsem)
        self.dnext += 1
        extra = [(("d", k), self.dcnt[k])] if self.dcnt[k] else []
        waits = self._deps(q, reads, writes, extra)
        self.dcnt[k] += 16
        src = (("d", k), self.dcnt[k])
        self.ops[q].append((lambda e: e.dma_start(out=out, in_=in_, **kw), waits, (self.dsem[k], 16)))
        self._mark(src, reads, writes)
        self.nops += 1
        return src

    def barrier(self):
        for e in self.ENG:
            waits = []
            for e2 in self.CE:
                if e2 == e and e == "pe":
                    continue
                v = self.cnt[e2]
                if v and self.waited[e].get(e2, 0) < v:
                    self.waited[e][e2] = v
                    waits.append((self.sem[e2], v))
            for k, v in enumerate(self.dcnt):
                key = ("d", k)
                if v and self.waited[e].get(key, 0) < v:
                    self.waited[e][key] = v
                    waits.append((self.dsem[k], v))
            if waits:
                self.ops[e].append((None, waits, None))

    def finish(self):
        waits = []
        for k, v in enumerate(self.dcnt):
            if v:
                waits.append((self.dsem[k], v))
        for e2 in self.CE:
            if self.cnt[e2]:
                waits.append((self.sem[e2], self.cnt[e2]))
        self.ops["sp"].append((None, waits, None))

    def emit(self):
        self.finish()
        nc = self.nc

        def mk(e):
            def body(eng):
                for fn, waits, inc in self.ops[e]:
                    for (s, v) in waits:
                        eng.wait_ge(s, v)
                    if fn is not None:
                        ins = fn(eng)
                        if inc is not None:
                            ins.then_inc(inc[0], inc[1])
            return body

        with nc.Block() as block:
            block.tensor(mk("pe"))
            block.vector(mk("dve"))
            block.scalar(mk("act"))
            block.gpsimd(mk("pool"))
            block.sync(mk("sp"))

from contextlib import ExitStack
import numpy as np

ALPHA = (2.0 * 4) ** 0.25
LN_EPS = 1e-5
D = 1024
DFF = 2816


def layer_norm_tile(P, y, gB, bB, o, stats, mv, tmp):
    for c in range(2):
        P.op("dve", lambda e, c=c: e.bn_stats(out=stats[:, c, :], in_=y[:, c * 512:(c + 1) * 512]),
             reads=[y], writes=[stats])
    P.op("dve", lambda e: e.bn_aggr(out=mv[:, :], in_=stats[:, :, :].rearrange("p a b -> p (a b)")),
         reads=[stats], writes=[mv])
    P.op("dve", lambda e: e.tensor_scalar(out=tmp[:, 0:1], in0=mv[:, 1:2], scalar1=LN_EPS, scalar2=None,
                                          op0=ALU.add), reads=[mv], writes=[tmp])
    P.op("act", lambda e: e.activation(out=tmp[:, 1:2], in_=tmp[:, 0:1], func=AF.Sqrt), reads=[tmp], writes=[tmp])
    P.op("dve", lambda e: e.reciprocal(out=tmp[:, 0:1], in_=tmp[:, 1:2]), reads=[tmp], writes=[tmp])
    P.op("dve", lambda e: e.tensor_scalar(out=o[:, :], in0=y[:, :], scalar1=mv[:, 0:1], scalar2=tmp[:, 0:1],
                                          op0=ALU.subtract, op1=ALU.mult), reads=[y, mv, tmp], writes=[o])
    P.op("pool", lambda e: e.tensor_tensor(out=o[:, :], in0=o[:, :], in1=gB[:, :], op=ALU.mult),
         reads=[o, gB], writes=[o])
    P.op("pool", lambda e: e.tensor_tensor(out=o[:, :], in0=o[:, :], in1=bB[:, :], op=ALU.add),
         reads=[o, bB], writes=[o])


def ffn_phase(P, h_d, w1r_d, w2r_d, g_d, b_d, out_d, ntok, ident, hbuf=None, obuf=None):
    nc = P.nc
    NP = 1024
    npass = ntok // NP
    with ExitStack() as es:
        hT = P.sb("hT", [128, 8, NP], BF16, es)
        aT = P.sb("aT", [128, 22, NP], BF16, es)
        w2b = P.sb("w2b", [128, 22, 1024], BF16, es)
        w2s = [P.sb("w2s%d" % i, [128, 1024], F32, es) for i in range(2)]
        w1s = [P.sb("w1s%d" % i, [128, 8, 512], F32, es) for i in range(2)]
        w1b = [P.sb("w1b%d" % i, [128, 8, 512], BF16, es) for i in range(2)]
        hin = [P.sb("hin%d" % i, [128, 1024], F32, es) for i in range(2)]
        h2 = [P.sb("h2_%d" % i, [128, 1024], F32, es) for i in range(2)]
        yb = [P.sb("yb%d" % i, [128, 1024], F32, es) for i in range(2)]
        ob = [P.sb("ob%d" % i, [128, 1024], F32, es) for i in range(2)]
        sg = [P.sb("sg%d" % i, [128, 512], F32, es) for i in range(2)]
        gB = P.sb("gB", [128, 1024], F32, es)
        bB = P.sb("bB", [128, 1024], F32, es)
        stats = [P.sb("st%d" % i, [128, 2, 6], F32, es) for i in range(2)]
        mv = [P.sb("mv%d" % i, [128, 2], F32, es) for i in range(2)]
        tmp = [P.sb("tm%d" % i, [128, 2], F32, es) for i in range(2)]
        pg = [P.ps("pg%d" % i, [128, 512], F32, es) for i in range(2)]
        pu = [P.ps("pu%d" % i, [128, 512], F32, es) for i in range(2)]
        pt = [P.ps("pt%d" % i, [128, 512], F32, es) for i in range(2)]
        po = [P.ps("po%d" % i, [128, 512], F32, es) for i in range(2)]

        P.dma(gB[:, :], g_d, writes=[gB])
        P.dma(bB[:, :], b_d, writes=[bB])
        for fc in range(22):
            s = w2s[fc % 2]
            P.dma(s[:, :], w2r_d[:, fc, :], writes=[s], q="pool")
            P.op("pool", lambda e, s=s, fc=fc: e.tensor_copy(out=w2b[:, fc, :], in_=s[:, :]), reads=[s], writes=[w2b])

        cnt = 0
        for ps_ in range(npass):
            t0 = ps_ * NP
            for tl in range(NP // 128):
                hb = hin[tl % 2]
                P.dma(hb[:, :], h_d[t0 + tl * 128:t0 + (tl + 1) * 128, :], reads=[hbuf], writes=[hb])
                for half in range(2):
                    p = pt[half]
                    for q4 in range(4):
                        kc = half * 4 + q4
                        P.op("pe", lambda e, p=p, q4=q4, kc=kc, hb=hb: e.transpose(
                            out=p[:, q4 * 128:(q4 + 1) * 128], in_=hb[:, kc * 128:(kc + 1) * 128], identity=ident[:, :]),
                            reads=[hb, ident], writes=[p])
                    P.op("act", lambda e, p=p, half=half, tl=tl: e.activation(
                        out=hT[:, half * 4:(half + 1) * 4, tl * 128:(tl + 1) * 128],
                        in_=p[:, :].rearrange("p (a b) -> p a b", a=4), func=AF.Copy),
                        reads=[p], writes=[hT])
            for j in range(11):
                s = w1s[j % 2]
                wb = w1b[j % 2]
                P.dma(s[:, :, :], w1r_d[j], writes=[s])
                P.op("pool", lambda e, s=s, wb=wb: e.tensor_copy(out=wb[:, :, :], in_=s[:, :, :]), reads=[s], writes=[wb])
                for fc in range(2):
                    for tg in range(NP // 512):
                        g_ = pg[cnt % 2]
                        u_ = pu[cnt % 2]
                        sg_ = sg[cnt % 2]
                        cnt += 1
                        for kc in range(8):
                            P.op("pe", lambda e, g_=g_, wb=wb, kc=kc, fc=fc, tg=tg: e.matmul(
                                g_[:, :], lhsT=wb[:, kc, fc * 128:(fc + 1) * 128], rhs=hT[:, kc, tg * 512:(tg + 1) * 512],
                                start=(kc == 0), stop=(kc == 7)), reads=[wb, hT], writes=[g_])
                        for kc in range(8):
                            P.op("pe", lambda e, u_=u_, wb=wb, kc=kc, fc=fc, tg=tg: e.matmul(
                                u_[:, :], lhsT=wb[:, kc, 256 + fc * 128:256 + (fc + 1) * 128],
                                rhs=hT[:, kc, tg * 512:(tg + 1) * 512],
                                start=(kc == 0), stop=(kc == 7)), reads=[wb, hT], writes=[u_])
                        P.op("act", lambda e, g_=g_, sg_=sg_: e.activation(out=sg_[:, :], in_=g_[:, :], func=AF.Silu),
                             reads=[g_], writes=[sg_])
                        P.op("dve", lambda e, u_=u_, sg_=sg_, j=j, fc=fc, tg=tg: e.tensor_tensor(
                            out=aT[:, 2 * j + fc, tg * 512:(tg + 1) * 512], in0=sg_[:, :], in1=u_[:, :], op=ALU.mult),
                            reads=[sg_, u_], writes=[aT])
            for tl in range(NP // 128):
                hb = hin[tl % 2]
                h2_ = h2[tl % 2]
                y_ = yb[tl % 2]
                o_ = ob[tl % 2]
                P.dma(hb[:, :], h_d[t0 + tl * 128:t0 + (tl + 1) * 128, :], reads=[hbuf], writes=[hb])
                P.op("pool", lambda e, hb=hb, h2_=h2_: e.tensor_scalar(out=h2_[:, :], in0=hb[:, :], scalar1=ALPHA,
                                                                       scalar2=None, op0=ALU.mult),
                     reads=[hb], writes=[h2_])
                for dh in range(2):
                    p = po[dh]
                    for fc in range(22):
                        P.op("pe", lambda e, p=p, fc=fc, tl=tl, dh=dh: e.matmul(
                            p[:, :], lhsT=aT[:, fc, tl * 128:(tl + 1) * 128], rhs=w2b[:, fc, dh * 512:(dh + 1) * 512],
                            start=(fc == 0), stop=(fc == 21)), reads=[aT, w2b], writes=[p])
                    P.op("dve", lambda e, p=p, dh=dh, y_=y_, h2_=h2_: e.scalar_tensor_tensor(
                        out=y_[:, dh * 512:(dh + 1) * 512], in0=p[:, :], scalar=0.5, in1=h2_[:, dh * 512:(dh + 1) * 512],
                        op0=ALU.mult, op1=ALU.add), reads=[p, h2_], writes=[y_])
                layer_norm_tile(P, y_, gB, bB, o_, stats[tl % 2], mv[tl % 2], tmp[tl % 2])
                P.dma(out_d[t0 + tl * 128:t0 + (tl + 1) * 128, :], o_[:, :], reads=[o_], writes=[obuf])
        P.barrier()


def prep_w1(w1):
    g = w1[:, :DFF].reshape(8, 128, 11, 256)
    u = w1[:, DFF:].reshape(8, 128, 11, 256)
    w = np.concatenate([g, u], axis=3)
    return np.ascontiguousarray(w.transpose(2, 1, 0, 3))


def prep_w2(w2):
    return np.ascontiguousarray(w2.reshape(22, 128, 1024).transpose(1, 0, 2))


def bc128(v):
    return np.ascontiguousarray(np.broadcast_to(v[None, :], (128, v.shape[0])))


NZ = 1816
ZCH = [(0, 512), (512, 512), (1024, 512), (1536, 280)]


def proj_phase(P, h_d, winr_d, cs_d, z_d, ntok, ident, hbuf=None, zbuf=None):
    with ExitStack() as es:
        wb = P.sb("winb", [128, 8, NZ], BF16, es)
        ws = [P.sb("wins%d" % i, [128, NZ], F32, es) for i in range(2)]
        hin = [P.sb("phin%d" % i, [128, 1024], F32, es) for i in range(2)]
        hT = [P.sb("phT%d" % i, [128, 8, 128], BF16, es) for i in range(2)]
        zo = [P.sb("zo%d" % i, [128, NZ], F32, es) for i in range(2)]
        cs = [P.sb("cs%d" % i, [128, 64], F32, es) for i in range(2)]
        tA = P.sb("rtA", [128, 256], F32, es)
        tB = P.sb("rtB", [128, 256], F32, es)
        tC = P.sb("rtC", [128, 256], F32, es)
        tD = P.sb("rtD", [128, 256], F32, es)
        pt = [P.ps("ppt%d" % i, [128, 512], F32, es) for i in range(2)]
        pz = [P.ps("ppz%d" % i, [128, 512], F32, es) for i in range(2)]
        for kc in range(8):
            s = ws[kc % 2]
            P.dma(s[:, :], winr_d[:, kc, :], writes=[s], q="pool")
            P.op("pool", lambda e, s=s, kc=kc: e.tensor_copy(out=wb[:, kc, :], in_=s[:, :]), reads=[s], writes=[wb])
        zc = 0
        for tl in range(ntok // 128):
            hb, hT_, zo_, cs_ = hin[tl % 2], hT[tl % 2], zo[tl % 2], cs[tl % 2]
            P.dma(hb[:, :], h_d[tl * 128:(tl + 1) * 128, :], reads=[hbuf], writes=[hb])
            P.dma(cs_[:, :], cs_d[tl * 128:(tl + 1) * 128, :], writes=[cs_])
            for half in range(2):
                p = pt[half]
                for q4 in range(4):
                    kc = half * 4 + q4
                    P.op("pe", lambda e, p=p, q4=q4, kc=kc, hb=hb: e.transpose(
                        out=p[:, q4 * 128:(q4 + 1) * 128], in_=hb[:, kc * 128:(kc + 1) * 128], identity=ident[:, :]),
                        reads=[hb, ident], writes=[p])
                P.op("act", lambda e, p=p, half=half, hT_=hT_: e.activation(
                    out=hT_[:, half * 4:(half + 1) * 4, :], in_=p[:, :].rearrange("p (a b) -> p a b", a=4), func=AF.Copy),
                    reads=[p], writes=[hT_])
            for (c0, w) in ZCH:
                p = pz[zc % 2]
                zc += 1
                for kc in range(8):
                    P.op("pe", lambda e, p=p, kc=kc, c0=c0, w=w, hT_=hT_: e.matmul(
                        p[:, 0:w], lhsT=hT_[:, kc, :], rhs=wb[:, kc, c0:c0 + w], start=(kc == 0), stop=(kc == 7)),
                        reads=[hT_, wb], writes=[p])
                if c0 == 1536:
                    P.op("act", lambda e, p=p, zo_=zo_: e.activation(out=zo_[:, 1536:1560], in_=p[:, 0:24], func=AF.Sigmoid),
                         reads=[p], writes=[zo_])
                    P.op("act", lambda e, p=p, zo_=zo_: e.activation(out=zo_[:, 1560:1816], in_=p[:, 24:280], func=AF.Copy),
                         reads=[p], writes=[zo_])
                else:
                    P.op("act", lambda e, p=p, zo_=zo_, c0=c0, w=w: e.activation(out=zo_[:, c0:c0 + w], in_=p[:, 0:w],
                                                                                func=AF.Copy), reads=[p], writes=[zo_])
            for (base, nh) in ((256, 8), (768, 2), (1024, 2), (1280, 2)):
                v = zo_[:, base:base + nh * 64].rearrange("p (h t e) -> p h t e", t=2, e=32)
                x1, x2 = v[:, :, 0, :], v[:, :, 1, :]
                cb = cs_[:, 0:32].unsqueeze(1).to_broadcast([128, nh, 32])
                sb_ = cs_[:, 32:64].unsqueeze(1).to_broadcast([128, nh, 32])
                a = tA[:, 0:nh * 32].rearrange("p (h e) -> p h e", e=32)
                b = tB[:, 0:nh * 32].rearrange("p (h e) -> p h e", e=32)
                c = tC[:, 0:nh * 32].rearrange("p (h e) -> p h e", e=32)
                d_ = tD[:, 0:nh * 32].rearrange("p (h e) -> p h e", e=32)
                P.op("dve", lambda e, a=a, x1=x1, cb=cb: e.tensor_tensor(out=a, in0=x1, in1=cb, op=ALU.mult),
                     reads=[zo_, cs_], writes=[tA])
                P.op("pool", lambda e, b=b, x2=x2, sb_=sb_: e.tensor_tensor(out=b, in0=x2, in1=sb_, op=ALU.mult),
                     reads=[zo_, cs_], writes=[tB])
                P.op("dve", lambda e, c=c, x2=x2, cb=cb: e.tensor_tensor(out=c, in0=x2, in1=cb, op=ALU.mult),
                     reads=[zo_, cs_], writes=[tC])
                P.op("pool", lambda e, d_=d_, x1=x1, sb_=sb_: e.tensor_tensor(out=d_, in0=x1, in1=sb_, op=ALU.mult),
                     reads=[zo_, cs_], writes=[tD])
                P.op("dve", lambda e, a=a, b=b, x1=x1: e.tensor_tensor(out=x1, in0=a, in1=b, op=ALU.subtract),
                     reads=[tA, tB], writes=[zo_])
                P.op("pool", lambda e, c=c, d_=d_, x2=x2: e.tensor_tensor(out=x2, in0=c, in1=d_, op=ALU.add),
                     reads=[tC, tD], writes=[zo_])
            P.dma(z_d[tl * 128:(tl + 1) * 128, :], zo_[:, :], reads=[zo_], writes=[zbuf])
        P.barrier()


def prep_win(w):
    return np.ascontiguousarray(w.reshape(8, 128, NZ).transpose(1, 0, 2))


def gelu_ops(P, x, t, o, n):
    P.op("dve", lambda e: e.tensor_tensor(out=t[:, 0:n], in0=x[:, 0:n], in1=x[:, 0:n], op=ALU.mult), reads=[x], writes=[t])
    P.op("pool", lambda e: e.tensor_scalar(out=t[:, 0:n], in0=t[:, 0:n], scalar1=0.044715, scalar2=1.0, op0=ALU.mult,
                                           op1=ALU.add), reads=[t], writes=[t])
    P.op("dve", lambda e: e.tensor_tensor(out=t[:, 0:n], in0=t[:, 0:n], in1=x[:, 0:n], op=ALU.mult), reads=[t, x], writes=[t])
    P.op("act", lambda e: e.activation(out=t[:, 0:n], in_=t[:, 0:n], func=AF.Sigmoid, scale=1.5957691216057308),
         reads=[t], writes=[t])
    P.op("pool", lambda e: e.tensor_tensor(out=o[:, 0:n], in0=t[:, 0:n], in1=x[:, 0:n], op=ALU.mult), reads=[t, x],
         writes=[o])


def mixout_phase(P, ntok, h1_d, d, g_d, b_d, out_d, ident, hbuf=None, obuf=None):
    W = 16 + ntok
    NG = ntok // 512
    with ExitStack() as es:
        mixT = P.sb("mixT", [128, 8, ntok], BF16, es)
        woutb = P.sb("woutb", [128, 8, 1024], BF16, es)
        stg = [P.sb("mstg%d" % i, [128, ntok], F32, es) for i in range(2)]
        X = P.sb("pX", [128, W], F32, es)
        sA = P.sb("psA", [128, W], F32, es)
        sB = P.sb("psB", [128, W], F32, es)
        sC = P.sb("psC", [128, W], F32, es)
        pooled = P.sb("pooled", [128, ntok], BF16, es)
        corr = P.sb("corr", [128, 2, 16], F32, es)
        invw = P.sb("invw", [128, 2], F32, es)
        pscale = P.sb("pscale", [128, 2], F32, es)
        pws = P.sb("pws", [128, 2, 128], F32, es)
        pwb = P.sb("pwb", [128, 2, 128], BF16, es)
        yg = [P.sb("yg%d" % i, [128, ntok], F32, es) for i in range(2)]
        ygb = P.sb("ygb", [128, 2, ntok], BF16, es)
        tt_ = P.sb("gtt", [128, ntok], F32, es)
        gls = P.sb("gls", [128, 2, 256], F32, es)
        glb = P.sb("glb", [128, 2, 256], BF16, es)
        glbias = P.sb("glbias", [128, 2], F32, es)
        sgm = [P.sb("sgm%d" % i, [128, 512], F32, es) for i in range(2)]
        gB = P.sb("mgB", [128, 1024], F32, es)
        bB = P.sb("mbB", [128, 1024], F32, es)
        hin = [P.sb("mhin%d" % i, [128, 1024], F32, es) for i in range(2)]
        h2 = [P.sb("mh2_%d" % i, [128, 1024], F32, es) for i in range(2)]
        yb = [P.sb("myb%d" % i, [128, 1024], F32, es) for i in range(2)]
        ob = [P.sb("mob%d" % i, [128, 1024], F32, es) for i in range(2)]
        stats = [P.sb("mst%d" % i, [128, 2, 6], F32, es) for i in range(2)]
        mv = [P.sb("mmv%d" % i, [128, 2], F32, es) for i in range(2)]
        tmp = [P.sb("mtm%d" % i, [128, 2], F32, es) for i in range(2)]
        pp = [P.ps("mpp%d" % i, [128, 512], F32, es) for i in range(2)]
        po = [P.ps("mpo%d" % i, [128, 512], F32, es) for i in range(2)]

        P.dma(gB[:, :], g_d, writes=[gB])
        P.dma(bB[:, :], b_d, writes=[bB])
        P.dma(corr[:, :, :], d["corr"], writes=[corr])
        P.dma(invw[:, :], d["invw"], writes=[invw])
        P.dma(pscale[:, :], d["pscale"], writes=[pscale])
        P.dma(glbias[:, :], d["glub"], writes=[glbias])
        P.dma(gls[:, :, :], d["gluw"], writes=[gls])
        P.op("pool", lambda e: e.tensor_copy(out=glb[:, :, :], in_=gls[:, :, :]), reads=[gls], writes=[glb])
        for ch in range(2):
            P.dma(pws[:, ch, :], d["pwblk"][ch], writes=[pws])
        P.op("pool", lambda e: e.tensor_copy(out=pwb[:, :, :], in_=pws[:, :, :]), reads=[pws], writes=[pwb])
        for kc in range(8):
            s = stg[kc % 2]
            P.dma(s[:, 0:1024], d["woutr"][:, kc, :], writes=[s])
            P.op("pool", lambda e, s=s, kc=kc: e.tensor_copy(out=woutb[:, kc, :], in_=s[:, 0:1024]), reads=[s],
                 writes=[woutb])
        for j in range(4):
            s = stg[j % 2]
            P.dma(s[:, :], d["onsaT"][j], writes=[s])
            P.op("pool", lambda e, s=s, j=j: e.tensor_copy(out=mixT[:, 2 + j, :], in_=s[:, :]), reads=[s], writes=[mixT])
        pcnt = 0
        for ch in range(2):
            P.dma(X[:, :], d["zpT"][ch], writes=[X])
            P.op("dve", lambda e: e.tensor_tensor(out=sA[:, 1:W], in0=X[:, 1:W], in1=X[:, 0:W - 1], op=ALU.add),
                 reads=[X], writes=[sA])
            P.op("pool", lambda e: e.tensor_tensor(out=sB[:, 3:W], in0=sA[:, 3:W], in1=sA[:, 1:W - 2], op=ALU.add),
                 reads=[sA], writes=[sB])
            if ch == 0:
                lo, hi_ = sA, sB
            else:
                P.op("dve", lambda e: e.tensor_tensor(out=sC[:, 7:W], in0=sB[:, 7:W], in1=sB[:, 3:W - 4], op=ALU.add),
                     reads=[sB], writes=[sC])
                P.op("pool", lambda e: e.tensor_tensor(out=sA[:, 15:W], in0=sC[:, 15:W], in1=sC[:, 7:W - 8], op=ALU.add),
                     reads=[sC], writes=[sA])
                lo, hi_ = sC, sA
            for (ps_, src) in ((slice(0, 64), lo), (slice(64, 128), hi_)):
                P.op("dve", lambda e, ps_=ps_, src=src, ch=ch: e.tensor_tensor(out=src[ps_, 16:32], in0=src[ps_, 16:32],
                                                                               in1=corr[ps_, ch, :], op=ALU.mult),
                     reads=[src, corr], writes=[src])
                P.op("dve", lambda e, ps_=ps_, src=src, ch=ch: e.scalar_tensor_tensor(
                    out=pooled[ps_, :], in0=src[ps_, 16:W], scalar=invw[ps_, ch:ch + 1], in1=X[ps_, 16:W], op0=ALU.mult,
                    op1=ALU.subtract), reads=[src, invw, X], writes=[pooled])
            for g4 in range(NG):
                p = pp[pcnt % 2]
                pcnt += 1
                P.op("pe", lambda e, p=p, ch=ch, g4=g4: e.matmul(p[:, :], lhsT=pwb[:, ch, :],
                                                                rhs=pooled[:, g4 * 512:(g4 + 1) * 512], start=True, stop=True),
                     reads=[pwb, pooled], writes=[p])
                P.op("act", lambda e, p=p, ch=ch, g4=g4: e.activation(out=mixT[:, ch, g4 * 512:(g4 + 1) * 512], in_=p[:, :],
                                                                     func=AF.Copy, scale=pscale[:, ch:ch + 1]),
                     reads=[p, pscale], writes=[mixT])
        for ch in range(2):
            s = stg[ch % 2]
            P.dma(s[:, :], d["yT"][ch], writes=[s])
            gelu_ops(P, s, tt_, yg[ch], ntok)
            P.op("pool", lambda e, ch=ch: e.tensor_copy(out=ygb[:, ch, :], in_=yg[ch][:, :]), reads=[yg[ch]], writes=[ygb])
        for oc in range(2):
            for g4 in range(NG):
                p = pp[pcnt % 2]
                sg_ = sgm[pcnt % 2]
                pcnt += 1
                for kc in range(2):
                    P.op("pe", lambda e, p=p, kc=kc, oc=oc, g4=g4: e.matmul(
                        p[:, :], lhsT=glb[:, kc, oc * 128:(oc + 1) * 128], rhs=ygb[:, kc, g4 * 512:(g4 + 1) * 512],
                        start=(kc == 0), stop=(kc == 1)), reads=[glb, ygb], writes=[p])
                P.op("act", lambda e, p=p, sg_=sg_, oc=oc: e.activation(out=sg_[:, :], in_=p[:, :], func=AF.Sigmoid,
                                                                        bias=glbias[:, oc:oc + 1]),
                     reads=[p, glbias], writes=[sg_])
                P.op("dve", lambda e, sg_=sg_, oc=oc, g4=g4: e.tensor_tensor(
                    out=mixT[:, 6 + oc, g4 * 512:(g4 + 1) * 512], in0=sg_[:, :], in1=yg[oc][:, g4 * 512:(g4 + 1) * 512],
                    op=ALU.mult), reads=[sg_, yg[oc]], writes=[mixT])
        for tl in range(ntok // 128):
            hb, h2_, y_, o_ = hin[tl % 2], h2[tl % 2], yb[tl % 2], ob[tl % 2]
            P.dma(hb[:, :], h1_d[tl * 128:(tl + 1) * 128, :], reads=[hbuf], writes=[hb])
            P.op("pool", lambda e, hb=hb, h2_=h2_: e.tensor_scalar(out=h2_[:, :], in0=hb[:, :], scalar1=ALPHA, scalar2=None,
                                                                   op0=ALU.mult), reads=[hb], writes=[h2_])
            for dh in range(2):
                p = po[dh]
                for kc in range(8):
                    P.op("pe", lambda e, p=p, kc=kc, tl=tl, dh=dh: e.matmul(
                        p[:, :], lhsT=mixT[:, kc, tl * 128:(tl + 1) * 128], rhs=woutb[:, kc, dh * 512:(dh + 1) * 512],
                        start=(kc == 0), stop=(kc == 7)), reads=[mixT, woutb], writes=[p])
                P.op("dve", lambda e, p=p, dh=dh, y_=y_, h2_=h2_: e.tensor_tensor(
                    out=y_[:, dh * 512:(dh + 1) * 512], in0=p[:, :], in1=h2_[:, dh * 512:(dh + 1) * 512], op=ALU.add),
                    reads=[p, h2_], writes=[y_])
            layer_norm_tile(P, y_, gB, bB, o_, stats[tl % 2], mv[tl % 2], tmp[tl % 2])
            P.dma(out_d[tl * 128:(tl + 1) * 128, :], o_[:, :], reads=[o_], writes=[obuf])
        P.barrier()

from contextlib import ExitStack
import numpy as np

SCALE = 64 ** -0.5
NEG = -1e30


def gelu_tanh(P, x, t, o):
    P.op("dve", lambda e: e.tensor_tensor(out=t[:, :], in0=x[:, :], in1=x[:, :], op=ALU.mult), reads=[x], writes=[t])
    P.op("dve", lambda e: e.tensor_scalar(out=t[:, :], in0=t[:, :], scalar1=0.044715, scalar2=1.0, op0=ALU.mult,
                                          op1=ALU.add), reads=[t], writes=[t])
    P.op("dve", lambda e: e.tensor_tensor(out=t[:, :], in0=t[:, :], in1=x[:, :], op=ALU.mult), reads=[t, x], writes=[t])
    P.op("act", lambda e: e.activation(out=t[:, :], in_=t[:, :], func=AF.Sigmoid, scale=1.5957691216057308),
         reads=[t], writes=[t])
    P.op("dve", lambda e: e.tensor_tensor(out=o[:, :], in0=t[:, :], in1=x[:, :], op=ALU.mult), reads=[t, x], writes=[o])


def nsa_phase(P, S, tiles, d, ident, out_d, obuf=None):
    nc = P.nc
    nt = S // 128
    ncmp = S // 16 - 1
    ncch = (ncmp + 127) // 128
    with ExitStack() as es:
        ksT = P.sb("ksT", [64, S], BF16, es)
        kwT = P.sb("kwT", [64, S], BF16, es)
        vs = P.sb("vs", [128, nt, 65], BF16, es)
        vw = P.sb("vw", [128, nt, 65], BF16, es)
        kcT = P.sb("kcT", [64, ncch * 128], F32, es)
        vcov = P.sb("vcov", [128, ncch, 65], F32, es)
        ov = P.sb("ov", [128, ncch, 128], F32, es)
        Eexp = P.sb("Eexp", [128, S], BF16, es)
        stg = [P.sb("stg%d" % i, [128, 2048], F32, es) for i in range(2)]
        pss = [P.ps("pss%d" % i, [128, 512], F32, es) for i in range(2)]
        pmt = P.ps("pmt", [128, 512], F32, es)
        pOs = P.ps("pOs", [128, 4, 65], F32, es)
        pOw = P.ps("pOw", [128, 4, 65], F32, es)
        pOc = P.ps("pOc", [128, 4, 65], F32, es)
        pUc = P.ps("pUc", [128, 4, 128], F32, es)
        pX = P.ps("pX", [128, 512], F32, es)

        P.dma(ov[:, :, :], d["ov"], writes=[ov])
        P.op("pool", lambda e: e.memset(Eexp[:, :], 1.0), writes=[Eexp])
        P.op("pool", lambda e: e.affine_select(out=Eexp[:, :], in_=Eexp[:, :], pattern=[[1, S]], compare_op=ALU.is_ge,
                                               fill=0.0, base=0, channel_multiplier=-64), reads=[Eexp], writes=[Eexp])
        P.op("pool", lambda e: e.affine_select(out=Eexp[:, :], in_=Eexp[:, :], pattern=[[-1, S]], compare_op=ALU.is_ge,
                                               fill=0.0, base=63, channel_multiplier=64), reads=[Eexp], writes=[Eexp])
        n2k = S // 2048 if S >= 2048 else 1
        w2k_ = min(S, 2048)
        ci = 0
        for src, dst in ((d["ksT"], ksT), (d["kwT"], kwT)):
            for c in range(n2k):
                s = stg[ci % 2]
                ci += 1
                P.dma(s[0:64, 0:w2k_], src[:, c * w2k_:(c + 1) * w2k_], writes=[s])
                P.op("pool", lambda e, s=s, dst=dst, c=c: e.tensor_copy(out=dst[:, c * w2k_:(c + 1) * w2k_],
                                                                         in_=s[0:64, 0:w2k_]), reads=[s], writes=[dst])
        for src, dst in ((d["vs"], vs), (d["vw"], vw)):
            P.op("pool", lambda e, dst=dst: e.memset(dst[:, :, 64:65], 1.0), writes=[dst])
            nper = 2048 // 64
            for c0 in range(0, nt, nper):
                n = min(nper, nt - c0)
                s = stg[ci % 2]
                ci += 1
                P.dma(s[:, 0:n * 64].rearrange("p (c e) -> p c e", e=64), src[:, c0:c0 + n, :], writes=[s])
                P.op("pool", lambda e, s=s, dst=dst, c0=c0, n=n: e.tensor_copy(
                    out=dst[:, c0:c0 + n, 0:64], in_=s[:, 0:n * 64].rearrange("p (c e) -> p c e", e=64)),
                    reads=[s], writes=[dst])
        with ExitStack() as es2:
            kin = P.sb("kin", [64, S], F32, es2)
            w1 = P.sb("w1", [64, 32, 128], F32, es2)
            w2 = P.sb("w2", [128, 64], F32, es2)
            pe = P.sb("pe", [64, 32], F32, es2)
            hb = P.sb("hb", [128, 1], F32, es2)
            H = P.sb("H", [128, 512], F32, es2)
            T = P.sb("T", [128, 512], F32, es2)
            G = P.sb("G", [128, 512], F32, es2)
            P.op("dve", lambda e: e.memset(G[:, :], 0.0), writes=[G])
            P.op("dve", lambda e: e.memset(kcT[:, :], 0.0), writes=[kcT])
            P.op("dve", lambda e: e.memset(vcov[:, :, :], 0.0), writes=[vcov])
            for which in ("k", "v"):
                P.dma(kin[:, :], d["kcT_in" if which == "k" else "vcT_in"], writes=[kin])
                P.dma(w1[:, :, :], d["w1" + which], writes=[w1])
                P.dma(w2[:, :], d["w2" + which], writes=[w2])
                P.dma(pe[:, :], d["pe" + which], writes=[pe])
                for pos in range(32):
                    P.op("pe", lambda e, pos=pos: e.matmul(pX[:, 0:1], lhsT=w1[:, pos, :], rhs=pe[:, pos:pos + 1],
                                                           start=(pos == 0), stop=(pos == 31)),
                         reads=[w1, pe], writes=[pX])
                P.op("act", lambda e: e.activation(out=hb[:, :], in_=pX[:, 0:1], func=AF.Copy), reads=[pX], writes=[hb])
                for pos in range(32):
                    P.op("pe", lambda e, pos=pos: e.matmul(pss[0][:, 0:ncmp], lhsT=w1[:, pos, :],
                                                           rhs=kin[:, pos:pos + 16 * (ncmp - 1) + 1:16],
                                                           start=(pos == 0), stop=(pos == 31)),
                         reads=[w1, kin], writes=[pss[0]])
                P.op("dve", lambda e: e.tensor_scalar(out=H[:, 0:ncmp], in0=pss[0][:, 0:ncmp], scalar1=hb[:, 0:1],
                                                      scalar2=None, op0=ALU.add), reads=[pss[0], hb], writes=[H])
                Hs = H
                P.op("dve", lambda e: e.tensor_tensor(out=T[:, 0:ncmp], in0=H[:, 0:ncmp], in1=H[:, 0:ncmp], op=ALU.mult),
                     reads=[H], writes=[T])
                P.op("dve", lambda e: e.tensor_scalar(out=T[:, 0:ncmp], in0=T[:, 0:ncmp], scalar1=0.044715, scalar2=1.0,
                                                      op0=ALU.mult, op1=ALU.add), reads=[T], writes=[T])
                P.op("dve", lambda e: e.tensor_tensor(out=T[:, 0:ncmp], in0=T[:, 0:ncmp], in1=H[:, 0:ncmp], op=ALU.mult),
                     reads=[T, H], writes=[T])
                P.op("act", lambda e: e.activation(out=T[:, 0:ncmp], in_=T[:, 0:ncmp], func=AF.Sigmoid,
                                                   scale=1.5957691216057308), reads=[T], writes=[T])
                P.op("dve", lambda e: e.tensor_tensor(out=G[:, 0:ncmp], in0=T[:, 0:ncmp], in1=H[:, 0:ncmp], op=ALU.mult),
                     reads=[T, H], writes=[G])
                if which == "k":
                    P.op("pe", lambda e: e.matmul(pss[1][0:64, 0:ncmp], lhsT=w2[:, :], rhs=G[:, 0:ncmp], start=True,
                                                  stop=True), reads=[w2, G], writes=[pss[1]])
                    P.op("act", lambda e: e.activation(out=kcT[:, 0:ncmp], in_=pss[1][0:64, 0:ncmp], func=AF.Copy),
                         reads=[pss[1]], writes=[kcT])
                else:
                    for m in range(ncch):
                        P.op("pe", lambda e, m=m: e.matmul(pss[1][:, 0:64], lhsT=G[:, m * 128:(m + 1) * 128],
                                                           rhs=w2[:, :], start=True, stop=True),
                             reads=[w2, G], writes=[pss[1]])
                        P.op("act", lambda e, m=m: e.activation(out=vcov[:, m, 0:64], in_=pss[1][:, 0:64], func=AF.Copy),
                             reads=[pss[1]], writes=[vcov])
            for m in range(ncch):
                n = min(128, ncmp - m * 128)
                P.op("dve", lambda e, m=m, n=n: e.memset(vcov[0:n, m, 64:65], 1.0), reads=[vcov], writes=[vcov])
            P.barrier()

        qf = [P.sb("qf%d" % i, [64, 4, 128], F32, es) for i in range(2)]
        qb = [P.sb("qb%d" % i, [64, 4, 128], BF16, es) for i in range(2)]
        gt = [P.sb("gt%d" % i, [128, 12], F32, es) for i in range(2)]
        bs = [P.sb("bs%d" % i, [128, 128], F32, es) for i in range(2)]
        Pc = [P.sb("Pc%d" % i, [128, 4, 128], F32, es) for i in range(3)]
        E = [P.sb("E%d" % i, [128, 4, 128], BF16, es) for i in range(3)]
        PT = [P.sb("PT%d" % i, [128, 4, 128], BF16, es) for i in range(3)]
        imp = P.sb("imp", [128, 128], F32, es)
        imp3 = P.sb("imp3", [128, 128], F32, es)
        sel = P.sb("sel", [128, 128], F32, es)
        selT = P.sb("selT", [128, 128], BF16, es)
        m8 = P.sb("m8", [128, 16], F32, es)
        den = P.sb("den", [128, 3, 4], F32, es)
        cf = P.sb("cf", [128, 3, 4], F32, es)
        acc = [P.sb("acc%d" % i, [128, 4, 64], F32, es) for i in range(2)]
        rot = {"s": 0, "pc": 0, "e": 0, "pt": 0}

        def nxt(lst, key):
            x = lst[rot[key] % len(lst)]
            rot[key] += 1
            return x

        for ti, i in enumerate(tiles):
            q0 = 128 * i
            qf_, qb_, gt_, bs_, acc_ = qf[ti % 2], qb[ti % 2], gt[ti % 2], bs[ti % 2], acc[ti % 2]
            P.dma(qf_[:, :, :], d["qT"][i], writes=[qf_])
            P.dma(gt_[:, :], d["gates"][q0:q0 + 128, :], writes=[gt_])
            P.dma(bs_[:, :], d["bias"][i], writes=[bs_])
            P.op("pool", lambda e, qf_=qf_, qb_=qb_: e.tensor_copy(out=qb_[:, :, :], in_=qf_[:, :, :]), reads=[qf_],
                 writes=[qb_])
            qf2 = qf_[:, :, :].rearrange("p a b -> p (a b)")
            qb2 = qb_[:, :, :].rearrange("p a b -> p (a b)")
            nvalid = min(8 * i + 7, ncmp)
            nch = (nvalid + 127) // 128
            for m in range(nch):
                s_ = nxt(pss, "s")
                pc_ = nxt(Pc, "pc")
                P.op("pe", lambda e, s_=s_, m=m, qf2=qf2: e.matmul(s_[:, :], lhsT=kcT[:, m * 128:(m + 1) * 128], rhs=qf2,
                                                                 start=True, stop=True), reads=[kcT, qf_], writes=[s_])
                P.op("act", lambda e, s_=s_, pc_=pc_: e.activation(out=pc_[:, :, :].rearrange("p a b -> p (a b)"),
                                                                   in_=s_[:, :], func=AF.Exp, scale=SCALE),
                     reads=[s_], writes=[pc_])
                if 128 * m + 127 >= 8 * i - 1:
                    P.op("pool", lambda e, pc_=pc_, m=m, i=i: e.affine_select(
                        out=pc_[:, :, :], in_=pc_[:, :, :], pattern=[[0, 4], [1, 128]], compare_op=ALU.is_ge, fill=0.0,
                        base=128 * i - 16 * 128 * m - 31, channel_multiplier=-16), reads=[pc_], writes=[pc_])
                for r in range(4):
                    P.op("pe", lambda e, pc_=pc_, m=m, r=r: e.matmul(pUc[:, r, :], lhsT=pc_[:, r, :], rhs=ov[:, m, :],
                                                                     start=(m == 0 and r == 0), stop=(m == nch - 1), skip_group_check=True),
                         reads=[pc_, ov], writes=[pUc])
                    P.op("pe", lambda e, pc_=pc_, m=m, r=r: e.matmul(pOc[:, r, :], lhsT=pc_[:, r, :], rhs=vcov[:, m, :],
                                                                     start=(m == 0 and r == 0), stop=(m == nch - 1), skip_group_check=True),
                         reads=[pc_, vcov], writes=[pOc])
            P.op("dve", lambda e: e.tensor_scalar(out=den[:, 0, :], in0=pOc[:, :, 64], scalar1=1e-30, scalar2=None,
                                                  op0=ALU.max), reads=[pOc], writes=[den])
            P.op("dve", lambda e: e.reciprocal(out=cf[:, 0, :], in_=den[:, 0, :]), reads=[den], writes=[cf])
            P.op("dve", lambda e: e.tensor_scalar(out=imp[:, :], in0=pUc[:, 0, :], scalar1=cf[:, 0, 0:1], scalar2=None,
                                                  op0=ALU.mult), reads=[pUc, cf], writes=[imp])
            for r in range(1, 4):
                P.op("dve", lambda e, r=r: e.scalar_tensor_tensor(out=imp[:, :], in0=pUc[:, r, :], scalar=cf[:, 0, r:r + 1],
                                                                  in1=imp[:, :], op0=ALU.mult, op1=ALU.add),
                     reads=[pUc, cf, imp], writes=[imp])
            P.op("dve", lambda e, bs_=bs_: e.tensor_tensor(out=imp[:, :], in0=imp[:, :], in1=bs_[:, :], op=ALU.add),
                 reads=[imp, bs_], writes=[imp])
            P.op("dve", lambda e: e.max(out=m8[:, 0:8], in_=imp[:, :]), reads=[imp], writes=[m8])
            P.op("dve", lambda e: e.match_replace(out=imp3[:, :], in_to_replace=m8[:, 0:8], in_values=imp[:, :],
                                                  imm_value=NEG), reads=[imp, m8], writes=[imp3])
            P.op("dve", lambda e: e.max(out=m8[:, 8:16], in_=imp3[:, :]), reads=[imp3], writes=[m8])
            P.op("dve", lambda e: e.tensor_scalar(out=sel[:, :], in0=imp[:, :], scalar1=m8[:, 15:16], scalar2=None,
                                                  op0=ALU.is_ge), reads=[imp, m8], writes=[sel])
            P.op("pe", lambda e: e.transpose(out=pX[:, 0:128], in_=sel[:, :], identity=ident[:, :]), reads=[sel, ident],
                 writes=[pX])
            P.op("act", lambda e: e.activation(out=selT[:, :], in_=pX[:, 0:128], func=AF.Copy), reads=[pX], writes=[selT])
            for kc in range(i + 1):
                s_ = nxt(pss, "s")
                e_ = nxt(E, "e")
                pt_ = nxt(PT, "pt")
                P.op("pe", lambda e, s_=s_, kc=kc, qb2=qb2: e.matmul(s_[:, 0:256], lhsT=ksT[:, kc * 128:(kc + 1) * 128], rhs=qb2[:, 0:256],
                                                                   start=True, stop=True), reads=[ksT, qb_], writes=[s_])
                P.op("act", lambda e, s_=s_, e_=e_: e.activation(out=e_[:, 0:2, :].rearrange("p a b -> p (a b)"),
                                                                 in_=s_[:, 0:256], func=AF.Exp, scale=SCALE),
                     reads=[s_], writes=[e_])
                P.op("pe", lambda e, kc=kc: e.matmul(pmt[:, 0:128], lhsT=Eexp[:, kc * 128:(kc + 1) * 128], rhs=selT[:, :],
                                                     start=True, stop=True), reads=[Eexp, selT], writes=[pmt])
                for r in range(2):
                    P.op("dve", lambda e, e_=e_, pt_=pt_, r=r: e.tensor_tensor(out=pt_[:, r, :], in0=e_[:, r, :],
                                                                               in1=pmt[:, 0:128], op=ALU.mult),
                         reads=[e_, pmt], writes=[pt_])
                if kc == i:
                    P.op("pool", lambda e, pt_=pt_: e.affine_select(
                        out=pt_[:, 0:2, :], in_=pt_[:, 0:2, :], pattern=[[0, 2], [1, 128]], compare_op=ALU.is_ge, fill=0.0,
                        base=0, channel_multiplier=-1), reads=[pt_], writes=[pt_])
                for r in range(2):
                    P.op("pe", lambda e, pt_=pt_, kc=kc, r=r, i=i: e.matmul(pOs[:, r, :], lhsT=pt_[:, r, :], rhs=vs[:, kc, :],
                                                                        start=(kc == 0 and r == 0), stop=(kc == i), skip_group_check=True),
                         reads=[pt_, vs], writes=[pOs])
            k0 = max(0, i - 4)
            for kc in range(k0, i + 1):
                s_ = nxt(pss, "s")
                e_ = nxt(E, "e")
                P.op("pe", lambda e, s_=s_, kc=kc, qb2=qb2: e.matmul(s_[:, 0:256], lhsT=kwT[:, kc * 128:(kc + 1) * 128], rhs=qb2[:, 0:256],
                                                                   start=True, stop=True), reads=[kwT, qb_], writes=[s_])
                P.op("act", lambda e, s_=s_, e_=e_: e.activation(out=e_[:, 0:2, :].rearrange("p a b -> p (a b)"),
                                                                 in_=s_[:, 0:256], func=AF.Exp, scale=SCALE),
                     reads=[s_], writes=[e_])
                if kc == i - 4:
                    P.op("pool", lambda e, e_=e_: e.affine_select(
                        out=e_[:, 0:2, :], in_=e_[:, 0:2, :], pattern=[[0, 2], [-1, 128]], compare_op=ALU.is_ge, fill=0.0,
                        base=-1, channel_multiplier=1), reads=[e_], writes=[e_])
                if kc == i:
                    P.op("pool", lambda e, e_=e_: e.affine_select(
                        out=e_[:, 0:2, :], in_=e_[:, 0:2, :], pattern=[[0, 2], [1, 128]], compare_op=ALU.is_ge, fill=0.0,
                        base=0, channel_multiplier=-1), reads=[e_], writes=[e_])
                for r in range(2):
                    P.op("pe", lambda e, e_=e_, kc=kc, r=r, i=i, k0=k0: e.matmul(pOw[:, r, :], lhsT=e_[:, r, :],
                                                                               rhs=vw[:, kc, :], start=(kc == k0 and r == 0),
                                                                               stop=(kc == i), skip_group_check=True),
                         reads=[e_, vw], writes=[pOw])
            P.op("dve", lambda e: e.memset(den[:, 1:3, 2:4], 1.0), reads=[den], writes=[den])
            P.op("dve", lambda e: e.tensor_copy(out=den[:, 1, 0:2], in_=pOs[:, 0:2, 64]), reads=[pOs], writes=[den])
            P.op("dve", lambda e: e.tensor_copy(out=den[:, 2, 0:2], in_=pOw[:, 0:2, 64]), reads=[pOw], writes=[den])
            P.op("dve", lambda e: e.reciprocal(out=cf[:, 1:3, :], in_=den[:, 1:3, :]), reads=[den], writes=[cf])
            P.op("dve", lambda e, gt_=gt_: e.tensor_tensor(out=cf[:, :, :], in0=cf[:, :, :],
                                                           in1=gt_[:, :].rearrange("p (r b) -> p b r", b=3), op=ALU.mult),
                 reads=[cf, gt_], writes=[cf])
            for r in range(2):
                P.op("dve", lambda e, r=r, acc_=acc_: e.tensor_scalar(out=acc_[:, r, :], in0=pOc[:, r, 0:64],
                                                                      scalar1=cf[:, 0, r:r + 1], scalar2=None, op0=ALU.mult),
                     reads=[pOc, cf], writes=[acc_])
                for bi, pO in ((1, pOs), (2, pOw)):
                    P.op("dve", lambda e, r=r, acc_=acc_, bi=bi, pO=pO: e.scalar_tensor_tensor(
                        out=acc_[:, r, :], in0=pO[:, r, 0:64], scalar=cf[:, bi, r:r + 1], in1=acc_[:, r, :], op0=ALU.mult,
                        op1=ALU.add), reads=[pO, cf, acc_], writes=[acc_])
            P.dma(out_d[ti * 128:(ti + 1) * 128, :], acc_[:, 0:2, :].rearrange("p a b -> p (a b)"), reads=[acc_],
                  writes=[obuf])
        P.barrier()


def nsa_consts(S):
    nt = S // 128
    ncmp = S // 16 - 1
    ncch = (ncmp + 127) // 128
    q = np.arange(128)
    bias = np.zeros((nt, 128, 128), np.float32)
    for i in range(nt):
        cur = (128 * i + q) // 64
        j = np.arange(128)
        forced = (j[None, :] == 0) | (j[None, :] == cur[:, None]) | (j[None, :] == cur[:, None] - 1)
        b = np.where(forced, 1e4, 0.0)
        b = np.where(j[None, :] > cur[:, None], NEG, b)
        bias[i] = b
    c = np.arange(ncch * 128)
    j = np.arange(128)
    ovm = ((16 * c[:, None] < 64 * j[None, :] + 64) & (16 * c[:, None] + 31 >= 64 * j[None, :]) & (c[:, None] < ncmp))
    ovm = ovm.astype(np.float32).reshape(ncch, 128, 128).transpose(1, 0, 2)
    return bias, np.ascontiguousarray(ovm)

from contextlib import ExitStack
import numpy as np

L = 512
LOGL = 9


def s5_phase(P, S, units, ident, obuf=None):
    nseg = S // L
    with ExitStack() as es:
        prm = P.sb("prm", [128, 4], F32, es)
        sc = P.sb("sc", [128, 24], F32, es)
        pw = P.sb("pw", [128, 2, LOGL + 6], F32, es)
        bre = P.sb("bre", [128, 16], F32, es)
        bim = P.sb("bim", [128, 16], F32, es)
        cre = P.sb("cre", [128, 16], F32, es)
        cim = P.sb("cim", [128, 16], F32, es)
        Bblk = P.sb("Bblk", [128, 2, 32], F32, es)
        Cblk = P.sb("Cblk", [128, 2, 32], F32, es)
        BT = P.sb("BT", [32, 2, 128], F32, es)
        dsk = P.sb("dsk", [32, 1], F32, es)
        uT = P.sb("uT", [32, S], F32, es)
        Ct = P.sb("Ct", [128, L], F32, es)
        St = P.sb("St", [128, L], F32, es)
        Rt = P.sb("Rt", [128, L], F32, es)
        tA = P.sb("tA", [128, L], F32, es)
        tB = P.sb("tB", [128, L], F32, es)
        tC = P.sb("tC", [128, L], F32, es)
        tD = P.sb("tD", [128, L], F32, es)
        xr = P.sb("xr", [128, L], F32, es)
        xi = P.sb("xi", [128, L], F32, es)
        gr = P.sb("gr", [128, L], F32, es)
        gi = P.sb("gi", [128, L], F32, es)
        hr = P.sb("hr", [128, L], F32, es)
        hi = P.sb("hi", [128, L], F32, es)
        ini = P.sb("ini", [128, 4], F32, es)
        ysb = [P.sb("ysb%d" % i, [32, L], F32, es) for i in range(2)]
        halfpi = P.sb("halfpi", [128, 1], F32, es)
        pxr = P.ps("pxr", [128, 512], F32, es)
        pxi = P.ps("pxi", [128, 512], F32, es)
        py = P.ps("py", [128, 512], F32, es)
        ptr = P.ps("ptr", [128, 512], F32, es)
        P.op("dve", lambda e: e.memset(halfpi[:, :], float(np.pi / 2)), writes=[halfpi])

        def ts(out, in0, s1, s2=None, op0=ALU.mult, op1=None, eng="dve", reads=(), writes=()):
            if op1 is None:
                P.op(eng, lambda e: e.tensor_scalar(out=out, in0=in0, scalar1=s1, scalar2=None, op0=op0), reads=reads,
                     writes=writes)
            else:
                P.op(eng, lambda e: e.tensor_scalar(out=out, in0=in0, scalar1=s1, scalar2=s2, op0=op0, op1=op1),
                     reads=reads, writes=writes)

        def tt(out, a, b, op, eng="dve", reads=(), writes=()):
            P.op(eng, lambda e: e.tensor_tensor(out=out, in0=a, in1=b, op=op), reads=reads, writes=writes)

        c_ = lambda k: sc[:, k:k + 1]
        for un in units:
            P.dma(prm[:, :], un["prm"], writes=[prm])
            P.dma(bre[:, :], un["bre"], writes=[bre])
            P.dma(bim[:, :], un["bim"], writes=[bim])
            P.dma(cre[:, :], un["cre"], writes=[cre])
            P.dma(cim[:, :], un["cim"], writes=[cim])
            P.dma(dsk[:, :], un["dsk"], writes=[dsk])
            P.dma(uT[:, :], un["uT"], writes=[uT])
            lr, li, ldt = prm[:, 0:1], prm[:, 1:2], prm[:, 2:3]
            P.op("act", lambda e: e.activation(out=c_(0), in_=ldt, func=AF.Exp), reads=[prm], writes=[sc])
            P.op("act", lambda e: e.activation(out=c_(1), in_=lr, func=AF.Exp, scale=c_(0)), reads=[prm, sc], writes=[sc])
            P.op("dve", lambda e: e.tensor_scalar(out=c_(2), in0=li, scalar1=c_(0), scalar2=1.0 / 32, op0=ALU.mult,
                                                  op1=ALU.mult), reads=[prm, sc], writes=[sc])
            P.op("act", lambda e: e.activation(out=pw[:, 1, 0:1], in_=c_(2), func=AF.Sin), reads=[sc], writes=[pw])
            P.op("act", lambda e: e.activation(out=pw[:, 0, 0:1], in_=c_(2), func=AF.Sin, bias=halfpi[:, 0:1]),
                 reads=[sc, halfpi], writes=[pw])
            for k in range(LOGL + 5):
                ck, sk = pw[:, 0, k:k + 1], pw[:, 1, k:k + 1]
                cn, sn = pw[:, 0, k + 1:k + 2], pw[:, 1, k + 1:k + 2]
                tt(c_(3), ck, ck, ALU.mult, reads=[pw], writes=[sc])
                tt(c_(4), sk, sk, ALU.mult, reads=[pw], writes=[sc])
                tt(cn, c_(3), c_(4), ALU.subtract, reads=[sc], writes=[pw])
                P.op("dve", lambda e, ck=ck, sk=sk, sn=sn: e.tensor_scalar(out=sn, in0=ck, scalar1=sk, scalar2=2.0,
                                                                         op0=ALU.mult, op1=ALU.mult), reads=[pw], writes=[pw])
            cth, sth = pw[:, 0, 5:6], pw[:, 1, 5:6]
            tt(c_(5), c_(1), cth, ALU.mult, reads=[sc, pw], writes=[sc])
            tt(c_(6), c_(1), sth, ALU.mult, reads=[sc, pw], writes=[sc])
            tt(c_(7), lr, lr, ALU.mult, reads=[prm], writes=[sc])
            P.op("dve", lambda e: e.scalar_tensor_tensor(out=c_(7), in0=li, scalar=li, in1=c_(7), op0=ALU.mult, op1=ALU.add),
                 reads=[prm, sc], writes=[sc])
            P.op("dve", lambda e: e.reciprocal(out=c_(8), in_=c_(7)), reads=[sc], writes=[sc])
            ts(c_(9), c_(5), -1.0, op0=ALU.add, reads=[sc], writes=[sc])
            tt(c_(10), c_(9), lr, ALU.mult, reads=[sc, prm], writes=[sc])
            P.op("dve", lambda e: e.scalar_tensor_tensor(out=c_(10), in0=c_(6), scalar=li, in1=c_(10), op0=ALU.mult,
                                                         op1=ALU.add), reads=[prm, sc], writes=[sc])
            tt(c_(10), c_(10), c_(8), ALU.mult, reads=[sc], writes=[sc])
            tt(c_(11), c_(9), li, ALU.mult, reads=[sc, prm], writes=[sc])
            P.op("dve", lambda e: e.scalar_tensor_tensor(out=c_(11), in0=c_(6), scalar=lr, in1=c_(11), op0=ALU.mult,
                                                         op1=ALU.subtract), reads=[prm, sc], writes=[sc])
            tt(c_(11), c_(11), c_(8), ALU.mult, reads=[sc], writes=[sc])
            ts(c_(12), c_(11), -1.0, op0=ALU.mult, reads=[sc], writes=[sc])
            P.op("dve", lambda e: e.memset(Bblk[:, :, :], 0.0), writes=[Bblk])
            P.op("dve", lambda e: e.memset(Cblk[:, :, :], 0.0), writes=[Cblk])
            for hf in range(2):
                ps_ = slice(64 * hf, 64 * hf + 64)
                cs_ = slice(16 * hf, 16 * hf + 16)
                P.op("dve", lambda e, ps_=ps_, cs_=cs_: e.tensor_scalar(out=Bblk[ps_, 0, cs_], in0=bre[ps_, :],
                                                                        scalar1=sc[ps_, 10:11], scalar2=None, op0=ALU.mult),
                     reads=[bre, sc], writes=[Bblk])
                P.op("dve", lambda e, ps_=ps_, cs_=cs_: e.scalar_tensor_tensor(
                    out=Bblk[ps_, 0, cs_], in0=bim[ps_, :], scalar=sc[ps_, 12:13], in1=Bblk[ps_, 0, cs_], op0=ALU.mult,
                    op1=ALU.add), reads=[bim, sc, Bblk], writes=[Bblk])
                P.op("dve", lambda e, ps_=ps_, cs_=cs_: e.tensor_scalar(out=Bblk[ps_, 1, cs_], in0=bim[ps_, :],
                                                                        scalar1=sc[ps_, 10:11], scalar2=None, op0=ALU.mult),
                     reads=[bim, sc], writes=[Bblk])
                P.op("dve", lambda e, ps_=ps_, cs_=cs_: e.scalar_tensor_tensor(
                    out=Bblk[ps_, 1, cs_], in0=bre[ps_, :], scalar=sc[ps_, 11:12], in1=Bblk[ps_, 1, cs_], op0=ALU.mult,
                    op1=ALU.add), reads=[bre, sc, Bblk], writes=[Bblk])
                P.op("dve", lambda e, ps_=ps_, cs_=cs_: e.tensor_copy(out=Cblk[ps_, 0, cs_], in_=cre[ps_, :]),
                     reads=[cre], writes=[Cblk])
                P.op("dve", lambda e, ps_=ps_, cs_=cs_: e.tensor_scalar(out=Cblk[ps_, 1, cs_], in0=cim[ps_, :], scalar1=-1.0,
                                                                        scalar2=None, op0=ALU.mult),
                     reads=[cim], writes=[Cblk])
            for ri in range(2):
                P.op("pe", lambda e, ri=ri: e.transpose(out=ptr[0:32, ri * 128:(ri + 1) * 128], in_=Bblk[:, ri, :],
                                                        identity=ident[:, :]), reads=[Bblk, ident], writes=[ptr])
            P.op("act", lambda e: e.activation(out=BT[:, :, :].rearrange("p a b -> p (a b)"), in_=ptr[0:32, 0:256],
                                               func=AF.Copy), reads=[ptr], writes=[BT])
            P.op("dve", lambda e: e.memset(Ct[:, 0:1], 1.0), writes=[Ct])
            P.op("dve", lambda e: e.memset(St[:, 0:1], 0.0), writes=[St])
            for k in range(LOGL):
                n = 1 << k
                ck, sk = pw[:, 0, 5 + k:6 + k], pw[:, 1, 5 + k:6 + k]
                ts(tA[:, 0:n], St[:, 0:n], sk, reads=[St, pw], writes=[tA])
                P.op("dve", lambda e, n=n, ck=ck: e.scalar_tensor_tensor(out=Ct[:, n:2 * n], in0=Ct[:, 0:n], scalar=ck,
                                                                         in1=tA[:, 0:n], op0=ALU.mult, op1=ALU.subtract),
                     reads=[Ct, pw, tA], writes=[Ct])
                ts(tB[:, 0:n], St[:, 0:n], ck, reads=[St, pw], writes=[tB])
                P.op("dve", lambda e, n=n, sk=sk: e.scalar_tensor_tensor(out=St[:, n:2 * n], in0=Ct[:, 0:n], scalar=sk,
                                                                         in1=tB[:, 0:n], op0=ALU.mult, op1=ALU.add),
                     reads=[Ct, pw, tB], writes=[St])
            cL, sL = pw[:, 0, 5 + LOGL:6 + LOGL], pw[:, 1, 5 + LOGL:6 + LOGL]
            P.op("dve", lambda e: e.memset(Rt[:, :], 1.0), writes=[Rt])
            ts(Rt[:, :], Rt[:, :], c_(1), reads=[Rt, sc], writes=[Rt])
            P.op("dve", lambda e: e.memset(ini[:, :], 0.0), writes=[ini])
            for sg in range(nseg):
                usl = uT[:, sg * L:(sg + 1) * L]
                P.op("pe", lambda e, usl=usl: e.matmul(pxr[:, :], lhsT=BT[:, 0, :], rhs=usl, start=True, stop=True),
                     reads=[BT, uT], writes=[pxr])
                P.op("pe", lambda e, usl=usl: e.matmul(pxi[:, :], lhsT=BT[:, 1, :], rhs=usl, start=True, stop=True),
                     reads=[BT, uT], writes=[pxi])
                P.op("act", lambda e: e.activation(out=xr[:, :], in_=pxr[:, :], func=AF.Copy), reads=[pxr], writes=[xr])
                P.op("act", lambda e: e.activation(out=xi[:, :], in_=pxi[:, :], func=AF.Copy), reads=[pxi], writes=[xi])
                tt(tA[:, :], xr[:, :], Ct[:, :], ALU.mult, "dve", [xr, Ct], [tA])
                tt(tB[:, :], xi[:, :], St[:, :], ALU.mult, "pool", [xi, St], [tB])
                tt(tC[:, :], xi[:, :], Ct[:, :], ALU.mult, "pool", [xi, Ct], [tC])
                tt(tD[:, :], xr[:, :], St[:, :], ALU.mult, "dve", [xr, St], [tD])
                tt(tA[:, :], tA[:, :], tB[:, :], ALU.add, "dve", [tA, tB], [tA])
                tt(tC[:, :], tC[:, :], tD[:, :], ALU.subtract, "pool", [tC, tD], [tC])
                P.op("dve", lambda e: e.tensor_tensor_scan(out=gr[:, :], data0=Rt[:, :], data1=tA[:, :], initial=ini[:, 0:1],
                                                           op0=ALU.mult, op1=ALU.add), reads=[Rt, tA, ini], writes=[gr])
                P.op("dve", lambda e: e.tensor_tensor_scan(out=gi[:, :], data0=Rt[:, :], data1=tC[:, :], initial=ini[:, 1:2],
                                                           op0=ALU.mult, op1=ALU.add), reads=[Rt, tC, ini], writes=[gi])
                ts(ini[:, 2:3], gi[:, L - 1:L], sL, reads=[gi, pw], writes=[ini])
                ts(ini[:, 3:4], gi[:, L - 1:L], cL, reads=[gi, pw], writes=[ini])
                P.op("dve", lambda e: e.scalar_tensor_tensor(out=ini[:, 0:1], in0=gr[:, L - 1:L], scalar=cL, in1=ini[:, 2:3],
                                                             op0=ALU.mult, op1=ALU.subtract), reads=[gr, pw, ini],
                     writes=[ini])
                P.op("dve", lambda e: e.scalar_tensor_tensor(out=ini[:, 1:2], in0=gr[:, L - 1:L], scalar=sL, in1=ini[:, 3:4],
                                                             op0=ALU.mult, op1=ALU.add), reads=[gr, pw, ini], writes=[ini])
                tt(tA[:, :], gr[:, :], Ct[:, :], ALU.mult, "dve", [gr, Ct], [tA])
                tt(tB[:, :], gi[:, :], St[:, :], ALU.mult, "pool", [gi, St], [tB])
                tt(tC[:, :], gr[:, :], St[:, :], ALU.mult, "pool", [gr, St], [tC])
                tt(tD[:, :], gi[:, :], Ct[:, :], ALU.mult, "dve", [gi, Ct], [tD])
                tt(hr[:, :], tA[:, :], tB[:, :], ALU.subtract, "dve", [tA, tB], [hr])
                tt(hi[:, :], tC[:, :], tD[:, :], ALU.add, "pool", [tC, tD], [hi])
                P.op("pe", lambda e: e.matmul(py[0:32, :], lhsT=Cblk[:, 0, :], rhs=hr[:, :], start=True, stop=False),
                     reads=[Cblk, hr], writes=[py])
                P.op("pe", lambda e: e.matmul(py[0:32, :], lhsT=Cblk[:, 1, :], rhs=hi[:, :], start=False, stop=True),
                     reads=[Cblk, hi], writes=[py])
                y_ = ysb[sg % 2]
                P.op("dve", lambda e, y_=y_, usl=usl: e.scalar_tensor_tensor(out=y_[:, :], in0=usl, scalar=dsk[:, 0:1],
                                                                           in1=py[0:32, :], op0=ALU.mult, op1=ALU.add),
                     reads=[uT, dsk, py], writes=[y_])
                P.dma(un["out"][:, sg * L:(sg + 1) * L], y_[:, :], reads=[y_], writes=[obuf])
        P.barrier()


def s5_host_unit(b_, scn, u_ssm, lam_re, lam_im, log_dt, b_re, b_im, c_re, c_im, dsk):
    gs = [2 * scn, 2 * scn + 1]
    prm = np.zeros((128, 4), np.float32)
    prm[:, 0] = lam_re[gs].reshape(-1)
    prm[:, 1] = lam_im[gs].reshape(-1)
    prm[:, 2] = np.repeat(log_dt[gs], 64)
    return dict(uT=np.ascontiguousarray(u_ssm[:, 32 * scn:32 * scn + 32].T), prm=prm,
                bre=np.ascontiguousarray(b_re[gs].reshape(128, 16)), bim=np.ascontiguousarray(b_im[gs].reshape(128, 16)),
                cre=np.ascontiguousarray(c_re[gs].transpose(0, 2, 1).reshape(128, 16)),
                cim=np.ascontiguousarray(c_im[gs].transpose(0, 2, 1).reshape(128, 16)),
                dsk=np.ascontiguousarray(dsk[gs].reshape(32, 1)))


from concourse.bass_utils import run_bass_kernel_spmd
import time as _time
import sys as _sys


def _run(prog, in_maps, tag):
    t = _time.time()
    res = run_bass_kernel_spmd(prog, in_maps, core_ids=list(range(8)))
    print("[kernel] launch %s %.1fs" % (tag, _time.time() - t), file=_sys.stderr, flush=True)
    return res

NCORES = 8
BATCH = 2
SEQ = 8192
NTOK = BATCH * SEQ // NCORES
DEPTH = 4
POOL_WINDOWS = (2, 4, 8, 16)


def _di(nc, name, shape):
    return nc.dram_tensor(name, list(shape), F32, kind="ExternalInput").ap()


def _do(nc, name, shape):
    return nc.dram_tensor(name, list(shape), F32, kind="ExternalOutput").ap()


def _dint(nc, name, shape):
    return nc.dram_tensor(name, list(shape), F32, kind="Internal").ap()


def _ffn_ins(nc, pfx):
    return dict(w1r=_di(nc, pfx + "w1r", [11, 128, 8, 512]), w2r=_di(nc, pfx + "w2r", [128, 22, 1024]),
                g=_di(nc, pfx + "g", [128, 1024]), b=_di(nc, pfx + "b", [128, 1024]))


def _mix_ins(nc):
    return dict(zpT=_di(nc, "zpT", [2, 128, 16 + NTOK]), corr=_di(nc, "corr", [128, 2, 16]), invw=_di(nc, "invw", [128, 2]),
                pwblk=_di(nc, "pwblk", [2, 128, 128]), pscale=_di(nc, "pscale", [128, 2]),
                onsaT=_di(nc, "onsaT", [4, 128, NTOK]), yT=_di(nc, "yT", [2, 128, NTOK]), gluw=_di(nc, "gluw", [128, 2, 256]),
                glub=_di(nc, "glub", [128, 2]), woutr=_di(nc, "woutr", [128, 8, 1024]), g=_di(nc, "mg", [128, 1024]),
                b=_di(nc, "mb", [128, 1024]))


def build_token_prog(first, last):
    nc = bass.Bass("TRN2", target_bir_lowering=False)
    idd = _di(nc, "ident", [128, 128])
    with ExitStack() as es:
        P = Prog(nc, es)
        ident = P.sb("ident", [128, 128], F32)
        P.dma(ident[:, :], idd, writes=[ident])
        if first:
            cur = _di(nc, "h_in", [NTOK, 1024])
            curb = None
        else:
            h1 = _di(nc, "h1_in", [NTOK, 1024])
            md = _mix_ins(nc)
            h2 = _dint(nc, "h2_int", [NTOK, 1024])
            h2b = Buf("h2")
            mixout_phase(P, NTOK, h1, md, md["g"], md["b"], h2, ident, None, h2b)
            f2 = _ffn_ins(nc, "f2_")
            if last:
                h3 = _do(nc, "h_out", [NTOK, 1024])
            else:
                h3 = _dint(nc, "h3_int", [NTOK, 1024])
            h3b = Buf("h3")
            ffn_phase(P, h2, f2["w1r"], f2["w2r"], f2["g"], f2["b"], h3, NTOK, ident, h2b, h3b)
            cur, curb = h3, h3b
        if not last:
            f1 = _ffn_ins(nc, "f1_")
            h1o = _do(nc, "h1_out", [NTOK, 1024])
            h1ob = Buf("h1o")
            ffn_phase(P, cur, f1["w1r"], f1["w2r"], f1["g"], f1["b"], h1o, NTOK, ident, curb, h1ob)
            winr = _di(nc, "winr", [128, 8, NZ])
            cs = _di(nc, "cs", [NTOK, 64])
            z = _do(nc, "z_out", [NTOK, NZ])
            proj_phase(P, h1o, winr, cs, z, NTOK, ident, h1ob, Buf("z"))
        P.emit()
    return nc


def build_mixer_prog():
    S = SEQ
    NT = S // 128
    NCCH = 4
    nc = bass.Bass("TRN2", target_bir_lowering=False)
    idd = _di(nc, "ident", [128, 128])
    d = dict(qT=_di(nc, "qT", [NT, 64, 4, 128]), kcT_in=_di(nc, "kcT_in", [64, S]), vcT_in=_di(nc, "vcT_in", [64, S]),
             ksT=_di(nc, "ksT", [64, S]), kwT=_di(nc, "kwT", [64, S]), vs=_di(nc, "vs", [128, NT, 64]),
             vw=_di(nc, "vw", [128, NT, 64]), gates=_di(nc, "gates", [S, 12]), w1k=_di(nc, "w1k", [64, 32, 128]),
             w2k=_di(nc, "w2k", [128, 64]), pek=_di(nc, "pek", [64, 32]), w1v=_di(nc, "w1v", [64, 32, 128]),
             w2v=_di(nc, "w2v", [128, 64]), pev=_di(nc, "pev", [64, 32]), bias=_di(nc, "bias", [NT, 128, 128]),
             ov=_di(nc, "ov", [128, NCCH, 128]))
    o = _do(nc, "o_nsa", [S, 128])
    units = []
    for u in range(2):
        units.append(dict(uT=_di(nc, "uT%d" % u, [32, S]), prm=_di(nc, "prm%d" % u, [128, 4]),
                          bre=_di(nc, "bre%d" % u, [128, 16]), bim=_di(nc, "bim%d" % u, [128, 16]),
                          cre=_di(nc, "cre%d" % u, [128, 16]), cim=_di(nc, "cim%d" % u, [128, 16]),
                          dsk=_di(nc, "dsk%d" % u, [32, 1]), out=_do(nc, "y%d" % u, [32, S])))
    with ExitStack() as es:
        P = Prog(nc, es)
        ident = P.sb("ident", [128, 128], F32)
        P.dma(ident[:, :], idd, writes=[ident])
        s5_phase(P, S, units, ident, Buf("y"))
        nsa_phase(P, S, list(range(NT)), d, ident, o, Buf("o"))
        P.emit()
    return nc


def _rope_cs():
    inv = (1.0 / (np.float32(10000.0) ** (np.arange(0, 64, 2, dtype=np.float32) / np.float32(64)))).astype(np.float32)
    ang = (np.arange(SEQ, dtype=np.float32)[:, None] * inv[None, :]).astype(np.float32)
    return np.ascontiguousarray(np.concatenate([np.cos(ang), np.sin(ang)], axis=1).astype(np.float32))


def _ffn_host(pfx, w_in, w_out, g, b):
    return {pfx + "w1r": prep_w1(w_in), pfx + "w2r": prep_w2(w_out), pfx + "g": bc128(g), pfx + "b": bc128(b)}


_CACHE = {}
_DBG = {}


def _prog(key, fn):
    if key not in _CACHE:
        _CACHE[key] = fn()
    return _CACHE[key]


def kernel(**inp):
    f32 = lambda a: np.ascontiguousarray(np.asarray(a, dtype=np.float32))
    x = f32(inp["x"]).reshape(BATCH * SEQ, 1024)
    ident = np.eye(128, dtype=np.float32)
    cs_full = _rope_cs()
    bias_c, ov_c = nsa_consts(SEQ)
    wch = np.array([POOL_WINDOWS[(c // 64)] for c in range(256)], np.float32)
    invw = np.ascontiguousarray((1.0 / wch).reshape(2, 128).T)
    t16 = np.arange(16, dtype=np.float32)
    corr_start = (wch[:, None] / np.minimum(t16[None, :] + 1.0, wch[:, None])).astype(np.float32)
    corr_start = np.ascontiguousarray(corr_start.reshape(2, 128, 16).transpose(1, 0, 2))
    corr_one = np.ones_like(corr_start)

    h1 = None
    z = None
    out = None
    cur = x
    for l in range(DEPTH + 1):
        first = (l == 0)
        last = (l == DEPTH)
        common = {"ident": ident}
        if not first:
            lm = l - 1
            zb = z.reshape(BATCH, SEQ, NZ)
            progB = _prog("B", build_mixer_prog)
            w1k = np.ascontiguousarray(f32(inp["cmp_k_w1"][lm]).reshape(32, 64, 128).transpose(1, 0, 2))
            w1v = np.ascontiguousarray(f32(inp["cmp_v_w1"][lm]).reshape(32, 64, 128).transpose(1, 0, 2))
            w2k = f32(inp["cmp_k_w2"][lm])
            w2v = f32(inp["cmp_v_w2"][lm])
            pek = np.ascontiguousarray(f32(inp["cmp_pe_k"][lm]).T)
            pev = np.ascontiguousarray(f32(inp["cmp_pe_v"][lm]).T)
            ssmw = [f32(inp["ssm_" + k][lm]) for k in ("lam_re", "lam_im", "log_dt", "b_re", "b_im", "c_re", "c_im", "d")]
            tok = lambda v: np.ascontiguousarray(v.reshape(SEQ // 128, 128, 64).transpose(1, 0, 2))
            in_maps = []
            for c in range(NCORES):
                b_, g_, rp = c // 4, (c // 2) % 2, c % 2
                hp = [4 * g_ + (2 * rp + k) % 4 for k in range(4)]
                zz = zb[b_]
                q = np.stack([zz[:, 256 + 64 * h:256 + 64 * h + 64] for h in hp], 0)
                kvs = [zz[:, 768 + 128 * s + 64 * g_:768 + 128 * s + 64 * g_ + 64] for s in range(6)]
                gts = np.concatenate([zz[:, 1536 + 3 * h:1536 + 3 * h + 3] for h in hp], 1)
                m = dict(common)
                m.update(qT=np.ascontiguousarray(q.reshape(4, SEQ // 128, 128, 64).transpose(1, 3, 0, 2)),
                         kcT_in=np.ascontiguousarray(kvs[0].T), vcT_in=np.ascontiguousarray(kvs[1].T),
                         ksT=np.ascontiguousarray(kvs[2].T), kwT=np.ascontiguousarray(kvs[4].T), vs=tok(kvs[3]),
                         vw=tok(kvs[5]), gates=np.ascontiguousarray(gts), w1k=w1k, w2k=w2k, pek=pek, w1v=w1v, w2v=w2v,
                         pev=pev, bias=bias_c, ov=ov_c)
                for uu in range(2):
                    unit = c * 2 + uu
                    ub, scn = unit // 8, unit % 8
                    hu = s5_host_unit(ub, scn, zb[ub][:, 1560:1816], *ssmw)
                    for k, v in hu.items():
                        m["%s%d" % (k, uu)] = v
                in_maps.append(m)
            res = _run(progB, in_maps, 'B%d' % lm)
            o_nsa = np.zeros((BATCH, SEQ, 512), np.float32)
            y_ssm = np.zeros((BATCH, SEQ, 256), np.float32)
            for c in range(NCORES):
                b_, g_, rp = c // 4, (c // 2) % 2, c % 2
                o_nsa[b_, :, 256 * g_ + 128 * rp:256 * g_ + 128 * rp + 128] = res.results[c]["o_nsa"]
                for uu in range(2):
                    unit = c * 2 + uu
                    ub, scn = unit // 8, unit % 8
                    y_ssm[ub, :, 32 * scn:32 * scn + 32] = res.results[c]["y%d" % uu].T
            _DBG['o_nsa_%d' % lm] = o_nsa
            _DBG['y_ssm_%d' % lm] = y_ssm
            pw = f32(inp["pool_w"][lm])
            pwblk = np.zeros((2, 128, 128), np.float32)
            for ch in range(2):
                for k in range(2):
                    pwblk[ch, 64 * k:64 * k + 64, 64 * k:64 * k + 64] = pw[2 * ch + k]
            common.update(invw=invw, pwblk=pwblk, pscale=np.ascontiguousarray(f32(inp["pool_scale"][lm]).reshape(2, 128).T),
                          gluw=np.ascontiguousarray(f32(inp["ssm_glu_w"][lm]).reshape(2, 128, 256).transpose(1, 0, 2)),
                          glub=np.ascontiguousarray(f32(inp["ssm_glu_b"][lm]).reshape(2, 128).T),
                          woutr=np.ascontiguousarray(f32(inp["w_out"][lm]).reshape(8, 128, 1024).transpose(1, 0, 2)),
                          mg=bc128(f32(inp["ln_g"][lm, 1])), mb=bc128(f32(inp["ln_b"][lm, 1])))
            common.update(_ffn_host("f2_", f32(inp["ffn2_in"][lm]), f32(inp["ffn2_out"][lm]), f32(inp["ln_g"][lm, 2]),
                                    f32(inp["ln_b"][lm, 2])))
        if not last:
            common.update(_ffn_host("f1_", f32(inp["ffn1_in"][l]), f32(inp["ffn1_out"][l]), f32(inp["ln_g"][l, 0]),
                                    f32(inp["ln_b"][l, 0])))
            common["winr"] = prep_win(f32(inp["w_in"][l]))
        in_maps = []
        for c in range(NCORES):
            b_, j = c // 4, c % 4
            t0 = j * NTOK
            m = dict(common)
            if first:
                m["h_in"] = np.ascontiguousarray(cur[c * NTOK:(c + 1) * NTOK])
            else:
                m["h1_in"] = np.ascontiguousarray(h1[c * NTOK:(c + 1) * NTOK])
                zp = np.zeros((16 + NTOK, 256), np.float32)
                lo = max(0, t0 - 16)
                zp[16 - (t0 - lo):] = zb[b_][lo:t0 + NTOK, 0:256]
                m["zpT"] = np.ascontiguousarray(zp.T.reshape(2, 128, 16 + NTOK))
                m["corr"] = corr_start if j == 0 else corr_one
                m["onsaT"] = np.ascontiguousarray(o_nsa[b_, t0:t0 + NTOK].T.reshape(4, 128, NTOK))
                m["yT"] = np.ascontiguousarray(y_ssm[b_, t0:t0 + NTOK].T.reshape(2, 128, NTOK))
            if not last:
                m["cs"] = np.ascontiguousarray(cs_full[t0:t0 + NTOK])
            in_maps.append(m)
        prog = _prog(("T", first, last), lambda: build_token_prog(first, last))
        res = _run(prog, in_maps, 'T%d' % l)
        if last:
            out = np.concatenate([res.results[c]["h_out"] for c in range(NCORES)], 0)
        else:
            h1 = np.concatenate([res.results[c]["h1_out"] for c in range(NCORES)], 0)
            z = np.concatenate([res.results[c]["z_out"] for c in range(NCORES)], 0)
            _DBG['h1_%d' % l] = h1
            _DBG['z_%d' % l] = z
    return out.reshape(BATCH, SEQ, 1024).astype(np.float32)
```

```python
from contextlib import ExitStack
import numpy as np
import numpy as np
import concourse.bass as bass
import concourse.mybir as mybir

F32 = mybir.dt.float32
BF16 = mybir.dt.bfloat16
AF = mybir.ActivationFunctionType
ALU = mybir.AluOpType
AX = mybir.AxisListType


class Buf:
    __slots__ = ("name", "w", "r")

    def __init__(self, name=""):
        self.name = name
        self.w = None
        self.r = {}


class TT:
    def __init__(self, t, name):
        self.t = t
        self.b = Buf(name)

    def __getitem__(self, k):
        return self.t[k]


class Prog:
    CE = ("pe", "dve", "act", "pool")
    ENG = ("pe", "dve", "act", "pool", "sp")

    def __init__(self, nc, es, n_dma_sems=24):
        self.nc = nc
        self.es = es
        self.ops = {e: [] for e in self.ENG}
        self.sem = {e: es.enter_context(nc.semaphore("s_" + e)) for e in self.CE}
        self.cnt = {e: 0 for e in self.CE}
        self.dsem = [es.enter_context(nc.semaphore("d%d" % i)) for i in range(n_dma_sems)]
        self.dcnt = [0] * n_dma_sems
        self.dnext = 0
        self.waited = {e: {} for e in self.ENG}
        self.nops = 0

    def sb(self, name, shape, dtype=F32, es=None):
        self.uid = getattr(self, "uid", 0) + 1
        t = (es or self.es).enter_context(self.nc.sbuf_tensor("sb%d_%s" % (self.uid, name), list(shape), dtype))
        return TT(t, name)

    def ps(self, name, shape, dtype=F32, es=None):
        self.uid = getattr(self, "uid", 0) + 1
        t = (es or self.es).enter_context(self.nc.psum_tensor("ps%d_%s" % (self.uid, name), list(shape), dtype))
        return TT(t, name)

    def _semh(self, key):
        if isinstance(key, tuple):
            return self.dsem[key[1]]
        return self.sem[key]

    def _deps(self, eng, reads, writes, extra=()):
        need = {}

        def add(src):
            if src is None:
                return
            key, val = src
            if need.get(key, 0) < val:
                need[key] = val

        for b in reads:
            add(b.w)
        for b in writes:
            add(b.w)
            for k, v in b.r.items():
                add((k, v))
        for s in extra:
            add(s)
        waits = []
        for key, val in need.items():
            if key == eng and eng == "pe":
                continue
            if self.waited[eng].get(key, 0) >= val:
                continue
            self.waited[eng][key] = val
            waits.append((self._semh(key), val))
        return waits

    def _mark(self, src, reads, writes):
        key, val = src
        for b in reads:
            if b.r.get(key, 0) < val:
                b.r[key] = val
        for b in writes:
            b.w = src
            b.r = {}

    @staticmethod
    def _bufs(xs):
        out = []
        for x in xs:
            if x is None:
                continue
            out.append(x.b if isinstance(x, TT) else x)
        return out

    def op(self, eng, fn, reads=(), writes=()):
        reads = self._bufs(reads)
        writes = self._bufs(writes)
        waits = self._deps(eng, reads, writes)
        self.cnt[eng] += 1
        src = (eng, self.cnt[eng])
        self.ops[eng].append((fn, waits, (self.sem[eng], 1)))
        self._mark(src, reads, writes)
        self.nops += 1
        return src

    def dma(self, out, in_, reads=(), writes=(), q="sp", **kw):
        reads = self._bufs(reads)
        writes = self._bufs(writes)
        k = self.dnext % len(self.dsem)
        self.dnext += 1
        extra = [(("d", k), self.dcnt[k])] if self.dcnt[k] else []
        waits = self._deps(q, reads, writes, extra)
        self.dcnt[k] += 16
        src = (("d", k), self.dcnt[k])
        self.ops[q].append((lambda e: e.dma_start(out=out, in_=in_, **kw), waits, (self.dsem[k], 16)))
        self._mark(src, reads, writes)
        self.nops += 1
        return src

    def barrier(self):
        for e in self.ENG:
            waits = []
            for e2 in self.CE:
                if e2 == e and e == "pe":
                    continue
                v = self.cnt[e2]
                if v and self.waited[e].get(e2, 0) < v:
                    self.waited[e][e2] = v
                    waits.append((self.sem[e2], v))
            for k, v in enumerate(self.dcnt):
                key = ("d", k)
                if v and self.waited[e].get(key, 0) < v:
                    self.waited[e][key] = v
                    waits.append((self.dsem[k], v))
            if waits:
                self.ops[e].append((None, waits, None))

    def finish(self):
        waits = []
        for k, v in enumerate(self.dcnt):
            if v:
                waits.append((self.dsem[k], v))
        for e2 in self.CE:
            if self.cnt[e2]:
                waits.append((self.sem[e2], self.cnt[e2]))
        self.ops["sp"].append((None, waits, None))

    def emit(self):
        self.finish()
        nc = self.nc

        def mk(e):
            def body(eng):
                for fn, waits, inc in self.ops[e]:
                    for (s, v) in waits:
                        eng.wait_ge(s, v)
                    if fn is not None:
                        ins = fn(eng)
                        if inc is not None:
                            ins.then_inc(inc[0], inc[1])
            return body

        with nc.Block() as block:
            block.tensor(mk("pe"))
            block.vector(mk("dve"))
            block.scalar(mk("act"))
            block.gpsimd(mk("pool"))
            block.sync(mk("sp"))

from contextlib import ExitStack
import numpy as np

ALPHA = (2.0 * 4) ** 0.25
LN_EPS = 1e-5
D = 1024
DFF = 2816


def layer_norm_tile(P, y, gB, bB, o, stats, mv, tmp):
    for c in range(2):
        P.op("dve", lambda e, c=c: e.bn_stats(out=stats[:, c, :], in_=y[:, c * 512:(c + 1) * 512]),
             reads=[y], writes=[stats])
    P.op("dve", lambda e: e.bn_aggr(out=mv[:, :], in_=stats[:, :, :].rearrange("p a b -> p (a b)")),
         reads=[stats], writes=[mv])
    P.op("dve", lambda e: e.tensor_scalar(out=tmp[:, 0:1], in0=mv[:, 1:2], scalar1=LN_EPS, scalar2=None,
                                          op0=ALU.add), reads=[mv], writes=[tmp])
    P.op("act", lambda e: e.activation(out=tmp[:, 1:2], in_=tmp[:, 0:1], func=AF.Sqrt), reads=[tmp], writes=[tmp])
    P.op("dve", lambda e: e.reciprocal(out=tmp[:, 0:1], in_=tmp[:, 1:2]), reads=[tmp], writes=[tmp])
    P.op("dve", lambda e: e.tensor_scalar(out=o[:, :], in0=y[:, :], scalar1=mv[:, 0:1], scalar2=tmp[:, 0:1],
                                          op0=ALU.subtract, op1=ALU.mult), reads=[y, mv, tmp], writes=[o])
    P.op("pool", lambda e: e.tensor_tensor(out=o[:, :], in0=o[:, :], in1=gB[:, :], op=ALU.mult),
         reads=[o, gB], writes=[o])
    P.op("pool", lambda e: e.tensor_tensor(out=o[:, :], in0=o[:, :], in1=bB[:, :], op=ALU.add),
         reads=[o, bB], writes=[o])


def ffn_phase(P, h_d, w1r_d, w2r_d, g_d, b_d, out_d, ntok, ident, hbuf=None, obuf=None):
    nc = P.nc
    NP = 1024
    npass = ntok // NP
    with ExitStack() as es:
        hT = P.sb("hT", [128, 8, NP], BF16, es)
        aT = P.sb("aT", [128, 22, NP], BF16, es)
        w2b = P.sb("w2b", [128, 22, 1024], BF16, es)
        w2s = [P.sb("w2s%d" % i, [128, 1024], F32, es) for i in range(2)]
        w1s = [P.sb("w1s%d" % i, [128, 8, 512], F32, es) for i in range(2)]
        w1b = [P.sb("w1b%d" % i, [128, 8, 512], BF16, es) for i in range(2)]
        hin = [P.sb("hin%d" % i, [128, 1024], F32, es) for i in range(2)]
        h2 = [P.sb("h2_%d" % i, [128, 1024], F32, es) for i in range(2)]
        yb = [P.sb("yb%d" % i, [128, 1024], F32, es) for i in range(2)]
        ob = [P.sb("ob%d" % i, [128, 1024], F32, es) for i in range(2)]
        sg = [P.sb("sg%d" % i, [128, 512], F32, es) for i in range(2)]
        gB = P.sb("gB", [128, 1024], F32, es)
        bB = P.sb("bB", [128, 1024], F32, es)
        stats = [P.sb("st%d" % i, [128, 2, 6], F32, es) for i in range(2)]
        mv = [P.sb("mv%d" % i, [128, 2], F32, es) for i in range(2)]
        tmp = [P.sb("tm%d" % i, [128, 2], F32, es) for i in range(2)]
        pg = [P.ps("pg%d" % i, [128, 512], F32, es) for i in range(2)]
        pu = [P.ps("pu%d" % i, [128, 512], F32, es) for i in range(2)]
        pt = [P.ps("pt%d" % i, [128, 512], F32, es) for i in range(2)]
        po = [P.ps("po%d" % i, [128, 512], F32, es) for i in range(2)]

        P.dma(gB[:, :], g_d, writes=[gB])
        P.dma(bB[:, :], b_d, writes=[bB])
        for fc in range(22):
            s = w2s[fc % 2]
            P.dma(s[:, :], w2r_d[:, fc, :], writes=[s], q="pool")
            P.op("pool", lambda e, s=s, fc=fc: e.tensor_copy(out=w2b[:, fc, :], in_=s[:, :]), reads=[s], writes=[w2b])

        cnt = 0
        for ps_ in range(npass):
            t0 = ps_ * NP
            for tl in range(NP // 128):
                hb = hin[tl % 2]
                P.dma(hb[:, :], h_d[t0 + tl * 128:t0 + (tl + 1) * 128, :], reads=[hbuf], writes=[hb])
                for half in range(2):
                    p = pt[half]
                    for q4 in range(4):
                        kc = half * 4 + q4
                        P.op("pe", lambda e, p=p, q4=q4, kc=kc, hb=hb: e.transpose(
                            out=p[:, q4 * 128:(q4 + 1) * 128], in_=hb[:, kc * 128:(kc + 1) * 128], identity=ident[:, :]),
                            reads=[hb, ident], writes=[p])
                    P.op("act", lambda e, p=p, half=half, tl=tl: e.activation(
                        out=hT[:, half * 4:(half + 1) * 4, tl * 128:(tl + 1) * 128],
                        in_=p[:, :].rearrange("p (a b) -> p a b", a=4), func=AF.Copy),
                        reads=[p], writes=[hT])
            for j in range(11):
                s = w1s[j % 2]
                wb = w1b[j % 2]
                P.dma(s[:, :, :], w1r_d[j], writes=[s])
                P.op("pool", lambda e, s=s, wb=wb: e.tensor_copy(out=wb[:, :, :], in_=s[:, :, :]), reads=[s], writes=[wb])
                for fc in range(2):
                    for tg in range(NP // 512):
                        g_ = pg[cnt % 2]
                        u_ = pu[cnt % 2]
                        sg_ = sg[cnt % 2]
                        cnt += 1
                        for kc in range(8):
                            P.op("pe", lambda e, g_=g_, wb=wb, kc=kc, fc=fc, tg=tg: e.matmul(
                                g_[:, :], lhsT=wb[:, kc, fc * 128:(fc + 1) * 128], rhs=hT[:, kc, tg * 512:(tg + 1) * 512],
                                start=(kc == 0), stop=(kc == 7)), reads=[wb, hT], writes=[g_])
                        for kc in range(8):
                            P.op("pe", lambda e, u_=u_, wb=wb, kc=kc, fc=fc, tg=tg: e.matmul(
                                u_[:, :], lhsT=wb[:, kc, 256 + fc * 128:256 + (fc + 1) * 128],
                                rhs=hT[:, kc, tg * 512:(tg + 1) * 512],
                                start=(kc == 0), stop=(kc == 7)), reads=[wb, hT], writes=[u_])
                        P.op("act", lambda e, g_=g_, sg_=sg_: e.activation(out=sg_[:, :], in_=g_[:, :], func=AF.Silu),
                             reads=[g_], writes=[sg_])
                        P.op("dve", lambda e, u_=u_, sg_=sg_, j=j, fc=fc, tg=tg: e.tensor_tensor(
                            out=aT[:, 2 * j + fc, tg * 512:(tg + 1) * 512], in0=sg_[:, :], in1=u_[:, :], op=ALU.mult),
                            reads=[sg_, u_], writes=[aT])
            for tl in range(NP // 128):
                hb = hin[tl % 2]
                h2_ = h2[tl % 2]
                y_ = yb[tl % 2]
                o_ = ob[tl % 2]
                P.dma(hb[:, :], h_d[t0 + tl * 128:t0 + (tl + 1) * 128, :], reads=[hbuf], writes=[hb])
                P.op("pool", lambda e, hb=hb, h2_=h2_: e.tensor_scalar(out=h2_[:, :], in0=hb[:, :], scalar1=ALPHA,
                                                                       scalar2=None, op0=ALU.mult),
                     reads=[hb], writes=[h2_])
                for dh in range(2):
                    p = po[dh]
                    for fc in range(22):
                        P.op("pe", lambda e, p=p, fc=fc, tl=tl, dh=dh: e.matmul(
                            p[:, :], lhsT=aT[:, fc, tl * 128:(tl + 1) * 128], rhs=w2b[:, fc, dh * 512:(dh + 1) * 512],
                            start=(fc == 0), stop=(fc == 21)), reads=[aT, w2b], writes=[p])
                    P.op("dve", lambda e, p=p, dh=dh, y_=y_, h2_=h2_: e.scalar_tensor_tensor(
                        out=y_[:, dh * 512:(dh + 1) * 512], in0=p[:, :], scalar=0.5, in1=h2_[:, dh * 512:(dh + 1) * 512],
                        op0=ALU.mult, op1=ALU.add), reads=[p, h2_], writes=[y_])
                layer_norm_tile(P, y_, gB, bB, o_, stats[tl % 2], mv[tl % 2], tmp[tl % 2])
                P.dma(out_d[t0 + tl * 128:t0 + (tl + 1) * 128, :], o_[:, :], reads=[o_], writes=[obuf])
        P.barrier()


def prep_w1(w1):
    g = w1[:, :DFF].reshape(8, 128, 11, 256)
    u = w1[:, DFF:].reshape(8, 128, 11, 256)
    w = np.concatenate([g, u], axis=3)
    return np.ascontiguousarray(w.transpose(2, 1, 0, 3))


def prep_w2(w2):
    return np.ascontiguousarray(w2.reshape(22, 128, 1024).transpose(1, 0, 2))


def bc128(v):
    return np.ascontiguousarray(np.broadcast_to(v[None, :], (128, v.shape[0])))


NZ = 1816
ZCH = [(0, 512), (512, 512), (1024, 512), (1536, 280)]


def proj_phase(P, h_d, winr_d, cs_d, z_d, ntok, ident, hbuf=None, zbuf=None):
    with ExitStack() as es:
        wb = P.sb("winb", [128, 8, NZ], BF16, es)
        ws = [P.sb("wins%d" % i, [128, NZ], F32, es) for i in range(2)]
        hin = [P.sb("phin%d" % i, [128, 1024], F32, es) for i in range(2)]
        hT = [P.sb("phT%d" % i, [128, 8, 128], BF16, es) for i in range(2)]
        zo = [P.sb("zo%d" % i, [128, NZ], F32, es) for i in range(2)]
        cs = [P.sb("cs%d" % i, [128, 64], F32, es) for i in range(2)]
        tA = P.sb("rtA", [128, 256], F32, es)
        tB = P.sb("rtB", [128, 256], F32, es)
        tC = P.sb("rtC", [128, 256], F32, es)
        tD = P.sb("rtD", [128, 256], F32, es)
        pt = [P.ps("ppt%d" % i, [128, 512], F32, es) for i in range(2)]
        pz = [P.ps("ppz%d" % i, [128, 512], F32, es) for i in range(2)]
        for kc in range(8):
            s = ws[kc % 2]
            P.dma(s[:, :], winr_d[:, kc, :], writes=[s], q="pool")
            P.op("pool", lambda e, s=s, kc=kc: e.tensor_copy(out=wb[:, kc, :], in_=s[:, :]), reads=[s], writes=[wb])
        zc = 0
        for tl in range(ntok // 128):
            hb, hT_, zo_, cs_ = hin[tl % 2], hT[tl % 2], zo[tl % 2], cs[tl % 2]
            P.dma(hb[:, :], h_d[tl * 128:(tl + 1) * 128, :], reads=[hbuf], writes=[hb])
            P.dma(cs_[:, :], cs_d[tl * 128:(tl + 1) * 128, :], writes=[cs_])
            for half in range(2):
                p = pt[half]
                for q4 in range(4):
                    kc = half * 4 + q4
                    P.op("pe", lambda e, p=p, q4=q4, kc=kc, hb=hb: e.transpose(
                        out=p[:, q4 * 128:(q4 + 1) * 128], in_=hb[:, kc * 128:(kc + 1) * 128], identity=ident[:, :]),
                        reads=[hb, ident], writes=[p])
                P.op("act", lambda e, p=p, half=half, hT_=hT_: e.activation(
                    out=hT_[:, half * 4:(half + 1) * 4, :], in_=p[:, :].rearrange("p (a b) -> p a b", a=4), func=AF.Copy),
                    reads=[p], writes=[hT_])
            for (c0, w) in ZCH:
                p = pz[zc % 2]
                zc += 1
                for kc in range(8):
                    P.op("pe", lambda e, p=p, kc=kc, c0=c0, w=w, hT_=hT_: e.matmul(
                        p[:, 0:w], lhsT=hT_[:, kc, :], rhs=wb[:, kc, c0:c0 + w], start=(kc == 0), stop=(kc == 7)),
                        reads=[hT_, wb], writes=[p])
                if c0 == 1536:
                    P.op("act", lambda e, p=p, zo_=zo_: e.activation(out=zo_[:, 1536:1560], in_=p[:, 0:24], func=AF.Sigmoid),
                         reads=[p], writes=[zo_])
                    P.op("act", lambda e, p=p, zo_=zo_: e.activation(out=zo_[:, 1560:1816], in_=p[:, 24:280], func=AF.Copy),
                         reads=[p], writes=[zo_])
                else:
                    P.op("act", lambda e, p=p, zo_=zo_, c0=c0, w=w: e.activation(out=zo_[:, c0:c0 + w], in_=p[:, 0:w],
                                                                                func=AF.Copy), reads=[p], writes=[zo_])
            for (base, nh) in ((256, 8), (768, 2), (1024, 2), (1280, 2)):
                v = zo_[:, base:base + nh * 64].rearrange("p (h t e) -> p h t e", t=2, e=32)
                x1, x2 = v[:, :, 0, :], v[:, :, 1, :]
                cb = cs_[:, 0:32].unsqueeze(1).to_broadcast([128, nh, 32])
                sb_ = cs_[:, 32:64].unsqueeze(1).to_broadcast([128, nh, 32])
                a = tA[:, 0:nh * 32].rearrange("p (h e) -> p h e", e=32)
                b = tB[:, 0:nh * 32].rearrange("p (h e) -> p h e", e=32)
                c = tC[:, 0:nh * 32].rearrange("p (h e) -> p h e", e=32)
                d_ = tD[:, 0:nh * 32].rearrange("p (h e) -> p h e", e=32)
                P.op("dve", lambda e, a=a, x1=x1, cb=cb: e.tensor_tensor(out=a, in0=x1, in1=cb, op=ALU.mult),
                     reads=[zo_, cs_], writes=[tA])
                P.op("pool", lambda e, b=b, x2=x2, sb_=sb_: e.tensor_tensor(out=b, in0=x2, in1=sb_, op=ALU.mult),
                     reads=[zo_, cs_], writes=[tB])
                P.op("dve", lambda e, c=c, x2=x2, cb=cb: e.tensor_tensor(out=c, in0=x2, in1=cb, op=ALU.mult),
                     reads=[zo_, cs_], writes=[tC])
                P.op("pool", lambda e, d_=d_, x1=x1, sb_=sb_: e.tensor_tensor(out=d_, in0=x1, in1=sb_, op=ALU.mult),
                     reads=[zo_, cs_], writes=[tD])
                P.op("dve", lambda e, a=a, b=b, x1=x1: e.tensor_tensor(out=x1, in0=a, in1=b, op=ALU.subtract),
                     reads=[tA, tB], writes=[zo_])
                P.op("pool", lambda e, c=c, d_=d_, x2=x2: e.tensor_tensor(out=x2, in0=c, in1=d_, op=ALU.add),
                     reads=[tC, tD], writes=[zo_])
            P.dma(z_d[tl * 128:(tl + 1) * 128, :], zo_[:, :], reads=[zo_], writes=[zbuf])
        P.barrier()


def prep_win(w):
    return np.ascontiguousarray(w.reshape(8, 128, NZ).transpose(1, 0, 2))


def gelu_ops(P, x, t, o, n):
    P.op("dve", lambda e: e.tensor_tensor(out=t[:, 0:n], in0=x[:, 0:n], in1=x[:, 0:n], op=ALU.mult), reads=[x], writes=[t])
    P.op("pool", lambda e: e.tensor_scalar(out=t[:, 0:n], in0=t[:, 0:n], scalar1=0.044715, scalar2=1.0, op0=ALU.mult,
                                           op1=ALU.add), reads=[t], writes=[t])
    P.op("dve", lambda e: e.tensor_tensor(out=t[:, 0:n], in0=t[:, 0:n], in1=x[:, 0:n], op=ALU.mult), reads=[t, x], writes=[t])
    P.op("act", lambda e: e.activation(out=t[:, 0:n], in_=t[:, 0:n], func=AF.Sigmoid, scale=1.5957691216057308),
         reads=[t], writes=[t])
    P.op("pool", lambda e: e.tensor_tensor(out=o[:, 0:n], in0=t[:, 0:n], in1=x[:, 0:n], op=ALU.mult), reads=[t, x],
         writes=[o])


def mixout_phase(P, ntok, h1_d, d, g_d, b_d, out_d, ident, hbuf=None, obuf=None):
    W = 16 + ntok
    NG = ntok // 512
    with ExitStack() as es:
        mixT = P.sb("mixT", [128, 8, ntok], BF16, es)
        woutb = P.sb("woutb", [128, 8, 1024], BF16, es)
        stg = [P.sb("mstg%d" % i, [128, ntok], F32, es) for i in range(2)]
        X = P.sb("pX", [128, W], F32, es)
        sA = P.sb("psA", [128, W], F32, es)
        sB = P.sb("psB", [128, W], F32, es)
        sC = P.sb("psC", [128, W], F32, es)
        pooled = P.sb("pooled", [128, ntok], BF16, es)
        corr = P.sb("corr", [128, 2, 16], F32, es)
        invw = P.sb("invw", [128, 2], F32, es)
        pscale = P.sb("pscale", [128, 2], F32, es)
        pws = P.sb("pws", [128, 2, 128], F32, es)
        pwb = P.sb("pwb", [128, 2, 128], BF16, es)
        yg = [P.sb("yg%d" % i, [128, ntok], F32, es) for i in range(2)]
        ygb = P.sb("ygb", [128, 2, ntok], BF16, es)
        tt_ = P.sb("gtt", [128, ntok], F32, es)
        gls = P.sb("gls", [128, 2, 256], F32, es)
        glb = P.sb("glb", [128, 2, 256], BF16, es)
        glbias = P.sb("glbias", [128, 2], F32, es)
        sgm = [P.sb("sgm%d" % i, [128, 512], F32, es) for i in range(2)]
        gB = P.sb("mgB", [128, 1024], F32, es)
        bB = P.sb("mbB", [128, 1024], F32, es)
        hin = [P.sb("mhin%d" % i, [128, 1024], F32, es) for i in range(2)]
        h2 = [P.sb("mh2_%d" % i, [128, 1024], F32, es) for i in range(2)]
        yb = [P.sb("myb%d" % i, [128, 1024], F32, es) for i in range(2)]
        ob = [P.sb("mob%d" % i, [128, 1024], F32, es) for i in range(2)]
        stats = [P.sb("mst%d" % i, [128, 2, 6], F32, es) for i in range(2)]
        mv = [P.sb("mmv%d" % i, [128, 2], F32, es) for i in range(2)]
        tmp = [P.sb("mtm%d" % i, [128, 2], F32, es) for i in range(2)]
        pp = [P.ps("mpp%d" % i, [128, 512], F32, es) for i in range(2)]
        po = [P.ps("mpo%d" % i, [128, 512], F32, es) for i in range(2)]

        P.dma(gB[:, :], g_d, writes=[gB])
        P.dma(bB[:, :], b_d, writes=[bB])
        P.dma(corr[:, :, :], d["corr"], writes=[corr])
        P.dma(invw[:, :], d["invw"], writes=[invw])
        P.dma(pscale[:, :], d["pscale"], writes=[pscale])
        P.dma(glbias[:, :], d["glub"], writes=[glbias])
        P.dma(gls[:, :, :], d["gluw"], writes=[gls])
        P.op("pool", lambda e: e.tensor_copy(out=glb[:, :, :], in_=gls[:, :, :]), reads=[gls], writes=[glb])
        for ch in range(2):
            P.dma(pws[:, ch, :], d["pwblk"][ch], writes=[pws])
        P.op("pool", lambda e: e.tensor_copy(out=pwb[:, :, :], in_=pws[:, :, :]), reads=[pws], writes=[pwb])
        for kc in range(8):
            s = stg[kc % 2]
            P.dma(s[:, 0:1024], d["woutr"][:, kc, :], writes=[s])
            P.op("pool", lambda e, s=s, kc=kc: e.tensor_copy(out=woutb[:, kc, :], in_=s[:, 0:1024]), reads=[s],
                 writes=[woutb])
        for j in range(4):
            s = stg[j % 2]
            P.dma(s[:, :], d["onsaT"][j], writes=[s])
            P.op("pool", lambda e, s=s, j=j: e.tensor_copy(out=mixT[:, 2 + j, :], in_=s[:, :]), reads=[s], writes=[mixT])
        pcnt = 0
        for ch in range(2):
            P.dma(X[:, :], d["zpT"][ch], writes=[X])
            P.op("dve", lambda e: e.tensor_tensor(out=sA[:, 1:W], in0=X[:, 1:W], in1=X[:, 0:W - 1], op=ALU.add),
                 reads=[X], writes=[sA])
            P.op("pool", lambda e: e.tensor_tensor(out=sB[:, 3:W], in0=sA[:, 3:W], in1=sA[:, 1:W - 2], op=ALU.add),
                 reads=[sA], writes=[sB])
            if ch == 0:
                lo, hi_ = sA, sB
            else:
                P.op("dve", lambda e: e.tensor_tensor(out=sC[:, 7:W], in0=sB[:, 7:W], in1=sB[:, 3:W - 4], op=ALU.add),
                     reads=[sB], writes=[sC])
                P.op("pool", lambda e: e.tensor_tensor(out=sA[:, 15:W], in0=sC[:, 15:W], in1=sC[:, 7:W - 8], op=ALU.add),
                     reads=[sC], writes=[sA])
                lo, hi_ = sC, sA
            for (ps_, src) in ((slice(0, 64), lo), (slice(64, 128), hi_)):
                P.op("dve", lambda e, ps_=ps_, src=src, ch=ch: e.tensor_tensor(out=src[ps_, 16:32], in0=src[ps_, 16:32],
                                                                               in1=corr[ps_, ch, :], op=ALU.mult),
                     reads=[src, corr], writes=[src])
                P.op("dve", lambda e, ps_=ps_, src=src, ch=ch: e.scalar_tensor_tensor(
                    out=pooled[ps_, :], in0=src[ps_, 16:W], scalar=invw[ps_, ch:ch + 1], in1=X[ps_, 16:W], op0=ALU.mult,
                    op1=ALU.subtract), reads=[src, invw, X], writes=[pooled])
            for g4 in range(NG):
                p = pp[pcnt % 2]
                pcnt += 1
                P.op("pe", lambda e, p=p, ch=ch, g4=g4: e.matmul(p[:, :], lhsT=pwb[:, ch, :],
                                                                rhs=pooled[:, g4 * 512:(g4 + 1) * 512], start=True, stop=True),
                     reads=[pwb, pooled], writes=[p])
                P.op("act", lambda e, p=p, ch=ch, g4=g4: e.activation(out=mixT[:, ch, g4 * 512:(g4 + 1) * 512], in_=p[:, :],
                                                                     func=AF.Copy, scale=pscale[:, ch:ch + 1]),
                     reads=[p, pscale], writes=[mixT])
        for ch in range(2):
            s = stg[ch % 2]
            P.dma(s[:, :], d["yT"][ch], writes=[s])
            gelu_ops(P, s, tt_, yg[ch], ntok)
            P.op("pool", lambda e, ch=ch: e.tensor_copy(out=ygb[:, ch, :], in_=yg[ch][:, :]), reads=[yg[ch]], writes=[ygb])
        for oc in range(2):
            for g4 in range(NG):
                p = pp[pcnt % 2]
                sg_ = sgm[pcnt % 2]
                pcnt += 1
                for kc in range(2):
                    P.op("pe", lambda e, p=p, kc=kc, oc=oc, g4=g4: e.matmul(
                        p[:, :], lhsT=glb[:, kc, oc * 128:(oc + 1) * 128], rhs=ygb[:, kc, g4 * 512:(g4 + 1) * 512],
                        start=(kc == 0), stop=(kc == 1)), reads=[glb, ygb], writes=[p])
                P.op("act", lambda e, p=p, sg_=sg_, oc=oc: e.activation(out=sg_[:, :], in_=p[:, :], func=AF.Sigmoid,
                                                                        bias=glbias[:, oc:oc + 1]),
                     reads=[p, glbias], writes=[sg_])
                P.op("dve", lambda e, sg_=sg_, oc=oc, g4=g4: e.tensor_tensor(
                    out=mixT[:, 6 + oc, g4 * 512:(g4 + 1) * 512], in0=sg_[:, :], in1=yg[oc][:, g4 * 512:(g4 + 1) * 512],
                    op=ALU.mult), reads=[sg_, yg[oc]], writes=[mixT])
        for tl in range(ntok // 128):
            hb, h2_, y_, o_ = hin[tl % 2], h2[tl % 2], yb[tl % 2], ob[tl % 2]
            P.dma(hb[:, :], h1_d[tl * 128:(tl + 1) * 128, :], reads=[hbuf], writes=[hb])
            P.op("pool", lambda e, hb=hb, h2_=h2_: e.tensor_scalar(out=h2_[:, :], in0=hb[:, :], scalar1=ALPHA, scalar2=None,
                                                                   op0=ALU.mult), reads=[hb], writes=[h2_])
            for dh in range(2):
                p = po[dh]
                for kc in range(8):
                    P.op("pe", lambda e, p=p, kc=kc, tl=tl, dh=dh: e.matmul(
                        p[:, :], lhsT=mixT[:, kc, tl * 128:(tl + 1) * 128], rhs=woutb[:, kc, dh * 512:(dh + 1) * 512],
                        start=(kc == 0), stop=(kc == 7)), reads=[mixT, woutb], writes=[p])
                P.op("dve", lambda e, p=p, dh=dh, y_=y_, h2_=h2_: e.tensor_tensor(
                    out=y_[:, dh * 512:(dh + 1) * 512], in0=p[:, :], in1=h2_[:, dh * 512:(dh + 1) * 512], op=ALU.add),
                    reads=[p, h2_], writes=[y_])
            layer_norm_tile(P, y_, gB, bB, o_, stats[tl % 2], mv[tl % 2], tmp[tl % 2])
            P.dma(out_d[tl * 128:(tl + 1) * 128, :], o_[:, :], reads=[o_], writes=[obuf])
        P.barrier()

from contextlib import ExitStack
import numpy as np

SCALE = 64 ** -0.5
NEG = -1e30


def gelu_tanh(P, x, t, o):
    P.op("dve", lambda e: e.tensor_tensor(out=t[:, :], in0=x[:, :], in1=x[:, :], op=ALU.mult), reads=[x], writes=[t])
    P.op("dve", lambda e: e.tensor_scalar(out=t[:, :], in0=t[:, :], scalar1=0.044715, scalar2=1.0, op0=ALU.mult,
                                          op1=ALU.add), reads=[t], writes=[t])
    P.op("dve", lambda e: e.tensor_tensor(out=t[:, :], in0=t[:, :], in1=x[:, :], op=ALU.mult), reads=[t, x], writes=[t])
    P.op("act", lambda e: e.activation(out=t[:, :], in_=t[:, :], func=AF.Sigmoid, scale=1.5957691216057308),
         reads=[t], writes=[t])
    P.op("dve", lambda e: e.tensor_tensor(out=o[:, :], in0=t[:, :], in1=x[:, :], op=ALU.mult), reads=[t, x], writes=[o])


def nsa_phase(P, S, tiles, d, ident, out_d, obuf=None):
    nc = P.nc
    nt = S // 128
    ncmp = S // 16 - 1
    ncch = (ncmp + 127) // 128
    with ExitStack() as es:
        ksT = P.sb("ksT", [64, S], BF16, es)
        kwT = P.sb("kwT", [64, S], BF16, es)
        vs = P.sb("vs", [128, nt, 65], BF16, es)
        vw = P.sb("vw", [128, nt, 65], BF16, es)
        kcT = P.sb("kcT", [64, ncch * 128], F32, es)
        vcov = P.sb("vcov", [128, ncch, 65], F32, es)
        ov = P.sb("ov", [128, ncch, 128], F32, es)
        Eexp = P.sb("Eexp", [128, S], BF16, es)
        stg = [P.sb("stg%d" % i, [128, 2048], F32, es) for i in range(2)]
        pss = [P.ps("pss%d" % i, [128, 512], F32, es) for i in range(2)]
        pmts = [P.ps("pmt%d" % i, [128, 512], F32, es) for i in range(2)]
        pO2 = P.ps("pO2", [128, 4, 65], F32, es)
        pOc = P.ps("pOc", [128, 4, 65], F32, es)
        pUc = P.ps("pUc", [128, 4, 128], F32, es)
        pX = P.ps("pX", [128, 512], F32, es)

        P.dma(ov[:, :, :], d["ov"], writes=[ov])
        P.op("pool", lambda e: e.memset(Eexp[:, :], 1.0), writes=[Eexp])
        P.op("pool", lambda e: e.affine_select(out=Eexp[:, :], in_=Eexp[:, :], pattern=[[1, S]], compare_op=ALU.is_ge,
                                               fill=0.0, base=0, channel_multiplier=-64), reads=[Eexp], writes=[Eexp])
        P.op("pool", lambda e: e.affine_select(out=Eexp[:, :], in_=Eexp[:, :], pattern=[[-1, S]], compare_op=ALU.is_ge,
                                               fill=0.0, base=63, channel_multiplier=64), reads=[Eexp], writes=[Eexp])
        n2k = S // 2048 if S >= 2048 else 1
        w2k_ = min(S, 2048)
        ci = 0
        for src, dst in ((d["ksT"], ksT), (d["kwT"], kwT)):
            for c in range(n2k):
                s = stg[ci % 2]
                ci += 1
                P.dma(s[0:64, 0:w2k_], src[:, c * w2k_:(c + 1) * w2k_], writes=[s])
                P.op("pool", lambda e, s=s, dst=dst, c=c: e.tensor_copy(out=dst[:, c * w2k_:(c + 1) * w2k_],
                                                                         in_=s[0:64, 0:w2k_]), reads=[s], writes=[dst])
        for src, dst in ((d["vs"], vs), (d["vw"], vw)):
            P.op("pool", lambda e, dst=dst: e.memset(dst[:, :, 64:65], 1.0), writes=[dst])
            nper = 2048 // 64
            for c0 in range(0, nt, nper):
                n = min(nper, nt - c0)
                s = stg[ci % 2]
                ci += 1
                P.dma(s[:, 0:n * 64].rearrange("p (c e) -> p c e", e=64), src[:, c0:c0 + n, :], writes=[s])
                P.op("pool", lambda e, s=s, dst=dst, c0=c0, n=n: e.tensor_copy(
                    out=dst[:, c0:c0 + n, 0:64], in_=s[:, 0:n * 64].rearrange("p (c e) -> p c e", e=64)),
                    reads=[s], writes=[dst])
        with ExitStack() as es2:
            kin = P.sb("kin", [64, S], F32, es2)
            w1 = P.sb("w1", [64, 32, 128], F32, es2)
            w2 = P.sb("w2", [128, 64], F32, es2)
            pe = P.sb("pe", [64, 32], F32, es2)
            hb = P.sb("hb", [128, 1], F32, es2)
            H = P.sb("H", [128, 512], F32, es2)
            T = P.sb("T", [128, 512], F32, es2)
            G = P.sb("G", [128, 512], F32, es2)
            P.op("dve", lambda e: e.memset(G[:, :], 0.0), writes=[G])
            P.op("dve", lambda e: e.memset(kcT[:, :], 0.0), writes=[kcT])
            P.op("dve", lambda e: e.memset(vcov[:, :, :], 0.0), writes=[vcov])
            for which in ("k", "v"):
                P.dma(kin[:, :], d["kcT_in" if which == "k" else "vcT_in"], writes=[kin])
                P.dma(w1[:, :, :], d["w1" + which], writes=[w1])
                P.dma(w2[:, :], d["w2" + which], writes=[w2])
                P.dma(pe[:, :], d["pe" + which], writes=[pe])
                for pos in range(32):
                    P.op("pe", lambda e, pos=pos: e.matmul(pX[:, 0:1], lhsT=w1[:, pos, :], rhs=pe[:, pos:pos + 1],
                                                           start=(pos == 0), stop=(pos == 31)),
                         reads=[w1, pe], writes=[pX])
                P.op("act", lambda e: e.activation(out=hb[:, :], in_=pX[:, 0:1], func=AF.Copy), reads=[pX], writes=[hb])
                for pos in range(32):
                    P.op("pe", lambda e, pos=pos: e.matmul(pss[0][:, 0:ncmp], lhsT=w1[:, pos, :],
                                                           rhs=kin[:, pos:pos + 16 * (ncmp - 1) + 1:16],
                                                           start=(pos == 0), stop=(pos == 31)),
                         reads=[w1, kin], writes=[pss[0]])
                P.op("dve", lambda e: e.tensor_scalar(out=H[:, 0:ncmp], in0=pss[0][:, 0:ncmp], scalar1=hb[:, 0:1],
                                                      scalar2=None, op0=ALU.add), reads=[pss[0], hb], writes=[H])
                Hs = H
                P.op("dve", lambda e: e.tensor_tensor(out=T[:, 0:ncmp], in0=H[:, 0:ncmp], in1=H[:, 0:ncmp], op=ALU.mult),
                     reads=[H], writes=[T])
                P.op("dve", lambda e: e.tensor_scalar(out=T[:, 0:ncmp], in0=T[:, 0:ncmp], scalar1=0.044715, scalar2=1.0,
                                                      op0=ALU.mult, op1=ALU.add), reads=[T], writes=[T])
                P.op("dve", lambda e: e.tensor_tensor(out=T[:, 0:ncmp], in0=T[:, 0:ncmp], in1=H[:, 0:ncmp], op=ALU.mult),
                     reads=[T, H], writes=[T])
                P.op("act", lambda e: e.activation(out=T[:, 0:ncmp], in_=T[:, 0:ncmp], func=AF.Sigmoid,
                                                   scale=1.5957691216057308), reads=[T], writes=[T])
                P.op("dve", lambda e: e.tensor_tensor(out=G[:, 0:ncmp], in0=T[:, 0:ncmp], in1=H[:, 0:ncmp], op=ALU.mult),
                     reads=[T, H], writes=[G])
                if which == "k":
                    P.op("pe", lambda e: e.matmul(pss[1][0:64, 0:ncmp], lhsT=w2[:, :], rhs=G[:, 0:ncmp], start=True,
                                                  stop=True), reads=[w2, G], writes=[pss[1]])
                    P.op("act", lambda e: e.activation(out=kcT[:, 0:ncmp], in_=pss[1][0:64, 0:ncmp], func=AF.Copy),
                         reads=[pss[1]], writes=[kcT])
                else:
                    for m in range(ncch):
                        P.op("pe", lambda e, m=m: e.matmul(pss[1][:, 0:64], lhsT=G[:, m * 128:(m + 1) * 128],
                                                           rhs=w2[:, :], start=True, stop=True),
                             reads=[w2, G], writes=[pss[1]])
                        P.op("act", lambda e, m=m: e.activation(out=vcov[:, m, 0:64], in_=pss[1][:, 0:64], func=AF.Copy),
                             reads=[pss[1]], writes=[vcov])
            for m in range(ncch):
                n = min(128, ncmp - m * 128)
                P.op("dve", lambda e, m=m, n=n: e.memset(vcov[0:n, m, 64:65], 1.0), reads=[vcov], writes=[vcov])
            P.barrier()

        qf = [P.sb("qf%d" % i, [64, 4, 128], F32, es) for i in range(2)]
        qb = [P.sb("qb%d" % i, [64, 4, 128], BF16, es) for i in range(2)]
        gt = [P.sb("gt%d" % i, [128, 12], F32, es) for i in range(2)]
        bs = [P.sb("bs%d" % i, [128, 128], F32, es) for i in range(2)]
        Pc = [P.sb("Pc%d" % i, [128, 4, 128], F32, es) for i in range(3)]
        E = [P.sb("E%d" % i, [128, 4, 128], BF16, es) for i in range(3)]
        PT = [P.sb("PT%d" % i, [128, 4, 128], BF16, es) for i in range(3)]
        imp = P.sb("imp", [128, 128], F32, es)
        imp3 = P.sb("imp3", [128, 128], F32, es)
        sel = P.sb("sel", [128, 128], F32, es)
        selT = P.sb("selT", [128, 128], BF16, es)
        m8 = P.sb("m8", [128, 16], F32, es)
        den = P.sb("den", [128, 3, 4], F32, es)
        cf = P.sb("cf", [128, 3, 4], F32, es)
        acc = [P.sb("acc%d" % i, [128, 4, 64], F32, es) for i in range(2)]
        rot = {"s": 0, "pc": 0, "e": 0, "pt": 0}

        def nxt(lst, key):
            x = lst[rot[key] % len(lst)]
            rot[key] += 1
            return x

        for ti, i in enumerate(tiles):
            q0 = 128 * i
            qf_, qb_, gt_, bs_, acc_ = qf[ti % 2], qb[ti % 2], gt[ti % 2], bs[ti % 2], acc[ti % 2]
            P.dma(qf_[:, :, :], d["qT"][i], writes=[qf_])
            P.dma(gt_[:, :], d["gates"][q0:q0 + 128, :], writes=[gt_])
            P.dma(bs_[:, :], d["bias"][i], writes=[bs_])
            P.op("pool", lambda e, qf_=qf_, qb_=qb_: e.tensor_copy(out=qb_[:, :, :], in_=qf_[:, :, :]), reads=[qf_],
                 writes=[qb_])
            qf2 = qf_[:, :, :].rearrange("p a b -> p (a b)")
            qb2 = qb_[:, :, :].rearrange("p a b -> p (a b)")
            nvalid = min(8 * i + 7, ncmp)
            nch = (nvalid + 127) // 128
            for m in range(nch):
                s_ = nxt(pss, "s")
                pc_ = nxt(Pc, "pc")
                P.op("pe", lambda e, s_=s_, m=m, qf2=qf2: e.matmul(s_[:, :], lhsT=kcT[:, m * 128:(m + 1) * 128], rhs=qf2,
                                                                 start=True, stop=True), reads=[kcT, qf_], writes=[s_])
                P.op("act", lambda e, s_=s_, pc_=pc_: e.activation(out=pc_[:, :, :].rearrange("p a b -> p (a b)"),
                                                                   in_=s_[:, :], func=AF.Exp, scale=SCALE),
                     reads=[s_], writes=[pc_])
                if 128 * m + 127 >= 8 * i - 1:
                    P.op("pool", lambda e, pc_=pc_, m=m, i=i: e.affine_select(
                        out=pc_[:, :, :], in_=pc_[:, :, :], pattern=[[0, 4], [1, 128]], compare_op=ALU.is_ge, fill=0.0,
                        base=128 * i - 16 * 128 * m - 31, channel_multiplier=-16), reads=[pc_], writes=[pc_])
                for r in range(4):
                    P.op("pe", lambda e, pc_=pc_, m=m, r=r: e.matmul(pUc[:, r, :], lhsT=pc_[:, r, :], rhs=ov[:, m, :],
                                                                     start=(m == 0 and r == 0), stop=(m == nch - 1), skip_group_check=True),
                         reads=[pc_, ov], writes=[pUc])
                    P.op("pe", lambda e, pc_=pc_, m=m, r=r: e.matmul(pOc[:, r, :], lhsT=pc_[:, r, :], rhs=vcov[:, m, :],
                                                                     start=(m == 0 and r == 0), stop=(m == nch - 1), skip_group_check=True),
                         reads=[pc_, vcov], writes=[pOc])
            P.op("dve", lambda e: e.tensor_scalar(out=den[:, 0, :], in0=pOc[:, :, 64], scalar1=1e-30, scalar2=None,
                                                  op0=ALU.max), reads=[pOc], writes=[den])
            P.op("dve", lambda e: e.reciprocal(out=cf[:, 0, :], in_=den[:, 0, :]), reads=[den], writes=[cf])
            P.op("dve", lambda e: e.tensor_scalar(out=imp[:, :], in0=pUc[:, 0, :], scalar1=cf[:, 0, 0:1], scalar2=None,
                                                  op0=ALU.mult), reads=[pUc, cf], writes=[imp])
            for r in range(1, 4):
                P.op("dve", lambda e, r=r: e.scalar_tensor_tensor(out=imp[:, :], in0=pUc[:, r, :], scalar=cf[:, 0, r:r + 1],
                                                                  in1=imp[:, :], op0=ALU.mult, op1=ALU.add),
                     reads=[pUc, cf, imp], writes=[imp])
            P.op("dve", lambda e, bs_=bs_: e.tensor_tensor(out=imp[:, :], in0=imp[:, :], in1=bs_[:, :], op=ALU.add),
                 reads=[imp, bs_], writes=[imp])
            P.op("dve", lambda e: e.max(out=m8[:, 0:8], in_=imp[:, :]), reads=[imp], writes=[m8])
            P.op("dve", lambda e: e.match_replace(out=imp3[:, :], in_to_replace=m8[:, 0:8], in_values=imp[:, :],
                                                  imm_value=NEG), reads=[imp, m8], writes=[imp3])
            P.op("dve", lambda e: e.max(out=m8[:, 8:16], in_=imp3[:, :]), reads=[imp3], writes=[m8])
            P.op("dve", lambda e: e.tensor_scalar(out=sel[:, :], in0=imp[:, :], scalar1=m8[:, 15:16], scalar2=None,
                                                  op0=ALU.is_ge), reads=[imp, m8], writes=[sel])
            P.op("pe", lambda e: e.transpose(out=pX[:, 0:128], in_=sel[:, :], identity=ident[:, :]), reads=[sel, ident],
                 writes=[pX])
            P.op("act", lambda e: e.activation(out=selT[:, :], in_=pX[:, 0:128], func=AF.Copy), reads=[pX], writes=[selT])
            for kc in range(i + 1):
                s_ = nxt(pss, "s")
                e_ = nxt(E, "e")
                pt_ = nxt(PT, "pt")
                P.op("pe", lambda e, s_=s_, kc=kc, qb2=qb2: e.matmul(s_[:, 0:256], lhsT=ksT[:, kc * 128:(kc + 1) * 128], rhs=qb2[:, 0:256],
                                                                   start=True, stop=True), reads=[ksT, qb_], writes=[s_])
                P.op("act", lambda e, s_=s_, e_=e_: e.activation(out=e_[:, 0:2, :].rearrange("p a b -> p (a b)"),
                                                                 in_=s_[:, 0:256], func=AF.Exp, scale=SCALE),
                     reads=[s_], writes=[e_])
                pmt = pmts[rot["pt"] % 2]
                P.op("pe", lambda e, kc=kc, pmt=pmt: e.matmul(pmt[:, 0:128], lhsT=Eexp[:, kc * 128:(kc + 1) * 128], rhs=selT[:, :],
                                                              start=True, stop=True), reads=[Eexp, selT], writes=[pmt])
                for r in range(2):
                    P.op("dve", lambda e, e_=e_, pt_=pt_, r=r, pmt=pmt: e.tensor_tensor(out=pt_[:, r, :], in0=e_[:, r, :],
                                                                                        in1=pmt[:, 0:128], op=ALU.mult),
                         reads=[e_, pmt], writes=[pt_])
                if kc == i:
                    P.op("pool", lambda e, pt_=pt_: e.affine_select(
                        out=pt_[:, 0:2, :], in_=pt_[:, 0:2, :], pattern=[[0, 2], [1, 128]], compare_op=ALU.is_ge, fill=0.0,
                        base=0, channel_multiplier=-1), reads=[pt_], writes=[pt_])
                for r in range(2):
                    P.op("pe", lambda e, pt_=pt_, kc=kc, r=r, i=i: e.matmul(pO2[:, r, :], lhsT=pt_[:, r, :], rhs=vs[:, kc, :],
                                                                        start=(kc == 0 and r == 0), stop=(kc == i), skip_group_check=True),
                         reads=[pt_, vs], writes=[pO2])
            k0 = max(0, i - 4)
            for kc in range(k0, i + 1):
                s_ = nxt(pss, "s")
                e_ = nxt(E, "e")
                P.op("pe", lambda e, s_=s_, kc=kc, qb2=qb2: e.matmul(s_[:, 0:256], lhsT=kwT[:, kc * 128:(kc + 1) * 128], rhs=qb2[:, 0:256],
                                                                   start=True, stop=True), reads=[kwT, qb_], writes=[s_])
                P.op("act", lambda e, s_=s_, e_=e_: e.activation(out=e_[:, 0:2, :].rearrange("p a b -> p (a b)"),
                                                                 in_=s_[:, 0:256], func=AF.Exp, scale=SCALE),
                     reads=[s_], writes=[e_])
                if kc == i - 4:
                    P.op("pool", lambda e, e_=e_: e.affine_select(
                        out=e_[:, 0:2, :], in_=e_[:, 0:2, :], pattern=[[0, 2], [-1, 128]], compare_op=ALU.is_ge, fill=0.0,
                        base=-1, channel_multiplier=1), reads=[e_], writes=[e_])
                if kc == i:
                    P.op("pool", lambda e, e_=e_: e.affine_select(
                        out=e_[:, 0:2, :], in_=e_[:, 0:2, :], pattern=[[0, 2], [1, 128]], compare_op=ALU.is_ge, fill=0.0,
                        base=0, channel_multiplier=-1), reads=[e_], writes=[e_])
                for r in range(2):
                    P.op("pe", lambda e, e_=e_, kc=kc, r=r, i=i, k0=k0: e.matmul(pO2[:, 2 + r, :], lhsT=e_[:, r, :],
                                                                               rhs=vw[:, kc, :], start=False,
                                                                               stop=(kc == i), skip_group_check=True),
                         reads=[e_, vw], writes=[pO2])
            P.op("dve", lambda e: e.memset(den[:, 1:3, 2:4], 1.0), reads=[den], writes=[den])
            P.op("dve", lambda e: e.tensor_copy(out=den[:, 1, 0:2], in_=pO2[:, 0:2, 64]), reads=[pO2], writes=[den])
            P.op("dve", lambda e: e.tensor_copy(out=den[:, 2, 0:2], in_=pO2[:, 2:4, 64]), reads=[pO2], writes=[den])
            P.op("dve", lambda e: e.reciprocal(out=cf[:, 1:3, :], in_=den[:, 1:3, :]), reads=[den], writes=[cf])
            P.op("dve", lambda e, gt_=gt_: e.tensor_tensor(out=cf[:, :, :], in0=cf[:, :, :],
                                                           in1=gt_[:, :].rearrange("p (r b) -> p b r", b=3), op=ALU.mult),
                 reads=[cf, gt_], writes=[cf])
            for r in range(2):
                P.op("dve", lambda e, r=r, acc_=acc_: e.tensor_scalar(out=acc_[:, r, :], in0=pOc[:, r, 0:64],
                                                                      scalar1=cf[:, 0, r:r + 1], scalar2=None, op0=ALU.mult),
                     reads=[pOc, cf], writes=[acc_])
                for bi in (1, 2):
                    P.op("dve", lambda e, r=r, acc_=acc_, bi=bi: e.scalar_tensor_tensor(
                        out=acc_[:, r, :], in0=pO2[:, 2 * (bi - 1) + r, 0:64], scalar=cf[:, bi, r:r + 1], in1=acc_[:, r, :],
                        op0=ALU.mult, op1=ALU.add), reads=[pO2, cf, acc_], writes=[acc_])
            P.dma(out_d[ti * 128:(ti + 1) * 128, :], acc_[:, 0:2, :].rearrange("p a b -> p (a b)"), reads=[acc_],
                  writes=[obuf])
        P.barrier()


def nsa_consts(S):
    nt = S // 128
    ncmp = S // 16 - 1
    ncch = (ncmp + 127) // 128
    q = np.arange(128)
    bias = np.zeros((nt, 128, 128), np.float32)
    for i in range(nt):
        cur = (128 * i + q) // 64
        j = np.arange(128)
        forced = (j[None, :] == 0) | (j[None, :] == cur[:, None]) | (j[None, :] == cur[:, None] - 1)
        b = np.where(forced, 1e4, 0.0)
        b = np.where(j[None, :] > cur[:, None], NEG, b)
        bias[i] = b
    c = np.arange(ncch * 128)
    j = np.arange(128)
    ovm = ((16 * c[:, None] < 64 * j[None, :] + 64) & (16 * c[:, None] + 31 >= 64 * j[None, :]) & (c[:, None] < ncmp))
    ovm = ovm.astype(np.float32).reshape(ncch, 128, 128).transpose(1, 0, 2)
    return bias, np.ascontiguousarray(ovm)

from contextlib import ExitStack
import numpy as np

L = 512
LOGL = 9


def s5_phase(P, S, units, ident, obuf=None):
    nseg = S // L
    with ExitStack() as es:
        prm = P.sb("prm", [128, 4], F32, es)
        sc = P.sb("sc", [128, 24], F32, es)
        pw = P.sb("pw", [128, 2, LOGL + 6], F32, es)
        bre = P.sb("bre", [128, 16], F32, es)
        bim = P.sb("bim", [128, 16], F32, es)
        cre = P.sb("cre", [128, 16], F32, es)
        cim = P.sb("cim", [128, 16], F32, es)
        Bblk = P.sb("Bblk", [128, 2, 32], F32, es)
        Cblk = P.sb("Cblk", [128, 2, 32], F32, es)
        BT = P.sb("BT", [32, 2, 128], F32, es)
        dsk = P.sb("dsk", [32, 1], F32, es)
        uT = P.sb("uT", [32, S], F32, es)
        Ct = P.sb("Ct", [128, L], F32, es)
        St = P.sb("St", [128, L], F32, es)
        Rt = P.sb("Rt", [128, L], F32, es)
        tA = P.sb("tA", [128, L], F32, es)
        tB = P.sb("tB", [128, L], F32, es)
        tC = P.sb("tC", [128, L], F32, es)
        tD = P.sb("tD", [128, L], F32, es)
        xr = P.sb("xr", [128, L], F32, es)
        xi = P.sb("xi", [128, L], F32, es)
        gr = P.sb("gr", [128, L], F32, es)
        gi = P.sb("gi", [128, L], F32, es)
        hr = P.sb("hr", [128, L], F32, es)
        hi = P.sb("hi", [128, L], F32, es)
        ini = P.sb("ini", [128, 4], F32, es)
        ysb = [P.sb("ysb%d" % i, [32, L], F32, es) for i in range(2)]
        halfpi = P.sb("halfpi", [128, 1], F32, es)
        pxr = P.ps("pxr", [128, 512], F32, es)
        pxi = P.ps("pxi", [128, 512], F32, es)
        py = P.ps("py", [128, 512], F32, es)
        ptr = P.ps("ptr", [128, 512], F32, es)
        P.op("dve", lambda e: e.memset(halfpi[:, :], float(np.pi / 2)), writes=[halfpi])

        def ts(out, in0, s1, s2=None, op0=ALU.mult, op1=None, eng="dve", reads=(), writes=()):
            if op1 is None:
                P.op(eng, lambda e: e.tensor_scalar(out=out, in0=in0, scalar1=s1, scalar2=None, op0=op0), reads=reads,
                     writes=writes)
            else:
                P.op(eng, lambda e: e.tensor_scalar(out=out, in0=in0, scalar1=s1, scalar2=s2, op0=op0, op1=op1),
                     reads=reads, writes=writes)

        def tt(out, a, b, op, eng="dve", reads=(), writes=()):
            P.op(eng, lambda e: e.tensor_tensor(out=out, in0=a, in1=b, op=op), reads=reads, writes=writes)

        c_ = lambda k: sc[:, k:k + 1]
        for un in units:
            P.dma(prm[:, :], un["prm"], writes=[prm])
            P.dma(bre[:, :], un["bre"], writes=[bre])
            P.dma(bim[:, :], un["bim"], writes=[bim])
            P.dma(cre[:, :], un["cre"], writes=[cre])
            P.dma(cim[:, :], un["cim"], writes=[cim])
            P.dma(dsk[:, :], un["dsk"], writes=[dsk])
            P.dma(uT[:, :], un["uT"], writes=[uT])
            lr, li, ldt = prm[:, 0:1], prm[:, 1:2], prm[:, 2:3]
            P.op("act", lambda e: e.activation(out=c_(0), in_=ldt, func=AF.Exp), reads=[prm], writes=[sc])
            P.op("act", lambda e: e.activation(out=c_(1), in_=lr, func=AF.Exp, scale=c_(0)), reads=[prm, sc], writes=[sc])
            P.op("dve", lambda e: e.tensor_scalar(out=c_(2), in0=li, scalar1=c_(0), scalar2=1.0 / 32, op0=ALU.mult,
                                                  op1=ALU.mult), reads=[prm, sc], writes=[sc])
            P.op("act", lambda e: e.activation(out=pw[:, 1, 0:1], in_=c_(2), func=AF.Sin), reads=[sc], writes=[pw])
            P.op("act", lambda e: e.activation(out=pw[:, 0, 0:1], in_=c_(2), func=AF.Sin, bias=halfpi[:, 0:1]),
                 reads=[sc, halfpi], writes=[pw])
            for k in range(LOGL + 5):
                ck, sk = pw[:, 0, k:k + 1], pw[:, 1, k:k + 1]
                cn, sn = pw[:, 0, k + 1:k + 2], pw[:, 1, k + 1:k + 2]
                tt(c_(3), ck, ck, ALU.mult, reads=[pw], writes=[sc])
                tt(c_(4), sk, sk, ALU.mult, reads=[pw], writes=[sc])
                tt(cn, c_(3), c_(4), ALU.subtract, reads=[sc], writes=[pw])
                P.op("dve", lambda e, ck=ck, sk=sk, sn=sn: e.tensor_scalar(out=sn, in0=ck, scalar1=sk, scalar2=2.0,
                                                                         op0=ALU.mult, op1=ALU.mult), reads=[pw], writes=[pw])
            cth, sth = pw[:, 0, 5:6], pw[:, 1, 5:6]
            tt(c_(5), c_(1), cth, ALU.mult, reads=[sc, pw], writes=[sc])
            tt(c_(6), c_(1), sth, ALU.mult, reads=[sc, pw], writes=[sc])
            tt(c_(7), lr, lr, ALU.mult, reads=[prm], writes=[sc])
            P.op("dve", lambda e: e.scalar_tensor_tensor(out=c_(7), in0=li, scalar=li, in1=c_(7), op0=ALU.mult, op1=ALU.add),
                 reads=[prm, sc], writes=[sc])
            P.op("dve", lambda e: e.reciprocal(out=c_(8), in_=c_(7)), reads=[sc], writes=[sc])
            ts(c_(9), c_(5), -1.0, op0=ALU.add, reads=[sc], writes=[sc])
            tt(c_(10), c_(9), lr, ALU.mult, reads=[sc, prm], writes=[sc])
            P.op("dve", lambda e: e.scalar_tensor_tensor(out=c_(10), in0=c_(6), scalar=li, in1=c_(10), op0=ALU.mult,
                                                         op1=ALU.add), reads=[prm, sc], writes=[sc])
            tt(c_(10), c_(10), c_(8), ALU.mult, reads=[sc], writes=[sc])
            tt(c_(11), c_(9), li, ALU.mult, reads=[sc, prm], writes=[sc])
            P.op("dve", lambda e: e.scalar_tensor_tensor(out=c_(11), in0=c_(6), scalar=lr, in1=c_(11), op0=ALU.mult,
                                                         op1=ALU.subtract), reads=[prm, sc], writes=[sc])
            tt(c_(11), c_(11), c_(8), ALU.mult, reads=[sc], writes=[sc])
            ts(c_(12), c_(11), -1.0, op0=ALU.mult, reads=[sc], writes=[sc])
            P.op("dve", lambda e: e.memset(Bblk[:, :, :], 0.0), writes=[Bblk])
            P.op("dve", lambda e: e.memset(Cblk[:, :, :], 0.0), writes=[Cblk])
            for hf in range(2):
                ps_ = slice(64 * hf, 64 * hf + 64)
                cs_ = slice(16 * hf, 16 * hf + 16)
                P.op("dve", lambda e, ps_=ps_, cs_=cs_: e.tensor_scalar(out=Bblk[ps_, 0, cs_], in0=bre[ps_, :],
                                                                        scalar1=sc[ps_, 10:11], scalar2=None, op0=ALU.mult),
                     reads=[bre, sc], writes=[Bblk])
                P.op("dve", lambda e, ps_=ps_, cs_=cs_: e.scalar_tensor_tensor(
                    out=Bblk[ps_, 0, cs_], in0=bim[ps_, :], scalar=sc[ps_, 12:13], in1=Bblk[ps_, 0, cs_], op0=ALU.mult,
                    op1=ALU.add), reads=[bim, sc, Bblk], writes=[Bblk])
                P.op("dve", lambda e, ps_=ps_, cs_=cs_: e.tensor_scalar(out=Bblk[ps_, 1, cs_], in0=bim[ps_, :],
                                                                        scalar1=sc[ps_, 10:11], scalar2=None, op0=ALU.mult),
                     reads=[bim, sc], writes=[Bblk])
                P.op("dve", lambda e, ps_=ps_, cs_=cs_: e.scalar_tensor_tensor(
                    out=Bblk[ps_, 1, cs_], in0=bre[ps_, :], scalar=sc[ps_, 11:12], in1=Bblk[ps_, 1, cs_], op0=ALU.mult,
                    op1=ALU.add), reads=[bre, sc, Bblk], writes=[Bblk])
                P.op("dve", lambda e, ps_=ps_, cs_=cs_: e.tensor_copy(out=Cblk[ps_, 0, cs_], in_=cre[ps_, :]),
                     reads=[cre], writes=[Cblk])
                P.op("dve", lambda e, ps_=ps_, cs_=cs_: e.tensor_scalar(out=Cblk[ps_, 1, cs_], in0=cim[ps_, :], scalar1=-1.0,
                                                                        scalar2=None, op0=ALU.mult),
                     reads=[cim], writes=[Cblk])
            for ri in range(2):
                P.op("pe", lambda e, ri=ri: e.transpose(out=ptr[0:32, ri * 128:(ri + 1) * 128], in_=Bblk[:, ri, :],
                                                        identity=ident[:, :]), reads=[Bblk, ident], writes=[ptr])
            P.op("act", lambda e: e.activation(out=BT[:, :, :].rearrange("p a b -> p (a b)"), in_=ptr[0:32, 0:256],
                                               func=AF.Copy), reads=[ptr], writes=[BT])
            P.op("dve", lambda e: e.memset(Ct[:, 0:1], 1.0), writes=[Ct])
            P.op("dve", lambda e: e.memset(St[:, 0:1], 0.0), writes=[St])
            for k in range(LOGL):
                n = 1 << k
                ck, sk = pw[:, 0, 5 + k:6 + k], pw[:, 1, 5 + k:6 + k]
                ts(tA[:, 0:n], St[:, 0:n], sk, reads=[St, pw], writes=[tA])
                P.op("dve", lambda e, n=n, ck=ck: e.scalar_tensor_tensor(out=Ct[:, n:2 * n], in0=Ct[:, 0:n], scalar=ck,
                                                                         in1=tA[:, 0:n], op0=ALU.mult, op1=ALU.subtract),
                     reads=[Ct, pw, tA], writes=[Ct])
                ts(tB[:, 0:n], St[:, 0:n], ck, reads=[St, pw], writes=[tB])
                P.op("dve", lambda e, n=n, sk=sk: e.scalar_tensor_tensor(out=St[:, n:2 * n], in0=Ct[:, 0:n], scalar=sk,
                                                                         in1=tB[:, 0:n], op0=ALU.mult, op1=ALU.add),
                     reads=[Ct, pw, tB], writes=[St])
            cL, sL = pw[:, 0, 5 + LOGL:6 + LOGL], pw[:, 1, 5 + LOGL:6 + LOGL]
            P.op("dve", lambda e: e.memset(Rt[:, :], 1.0), writes=[Rt])
            ts(Rt[:, :], Rt[:, :], c_(1), reads=[Rt, sc], writes=[Rt])
            P.op("dve", lambda e: e.memset(ini[:, :], 0.0), writes=[ini])
            for sg in range(nseg):
                usl = uT[:, sg * L:(sg + 1) * L]
                P.op("pe", lambda e, usl=usl: e.matmul(pxr[:, :], lhsT=BT[:, 0, :], rhs=usl, start=True, stop=True),
                     reads=[BT, uT], writes=[pxr])
                P.op("pe", lambda e, usl=usl: e.matmul(pxi[:, :], lhsT=BT[:, 1, :], rhs=usl, start=True, stop=True),
                     reads=[BT, uT], writes=[pxi])
                P.op("act", lambda e: e.activation(out=xr[:, :], in_=pxr[:, :], func=AF.Copy), reads=[pxr], writes=[xr])
                P.op("act", lambda e: e.activation(out=xi[:, :], in_=pxi[:, :], func=AF.Copy), reads=[pxi], writes=[xi])
                tt(tA[:, :], xr[:, :], Ct[:, :], ALU.mult, "dve", [xr, Ct], [tA])
                tt(tB[:, :], xi[:, :], St[:, :], ALU.mult, "pool", [xi, St], [tB])
                tt(tC[:, :], xi[:, :], Ct[:, :], ALU.mult, "pool", [xi, Ct], [tC])
                tt(tD[:, :], xr[:, :], St[:, :], ALU.mult, "dve", [xr, St], [tD])
                tt(tA[:, :], tA[:, :], tB[:, :], ALU.add, "dve", [tA, tB], [tA])
                tt(tC[:, :], tC[:, :], tD[:, :], ALU.subtract, "pool", [tC, tD], [tC])
                P.op("dve", lambda e: e.tensor_tensor_scan(out=gr[:, :], data0=Rt[:, :], data1=tA[:, :], initial=ini[:, 0:1],
                                                           op0=ALU.mult, op1=ALU.add), reads=[Rt, tA, ini], writes=[gr])
                P.op("dve", lambda e: e.tensor_tensor_scan(out=gi[:, :], data0=Rt[:, :], data1=tC[:, :], initial=ini[:, 1:2],
                                                           op0=ALU.mult, op1=ALU.add), reads=[Rt, tC, ini], writes=[gi])
                ts(ini[:, 2:3], gi[:, L - 1:L], sL, reads=[gi, pw], writes=[ini])
                ts(ini[:, 3:4], gi[:, L - 1:L], cL, reads=[gi, pw], writes=[ini])
                P.op("dve", lambda e: e.scalar_tensor_tensor(out=ini[:, 0:1], in0=gr[:, L - 1:L], scalar=cL, in1=ini[:, 2:3],
                                                             op0=ALU.mult, op1=ALU.subtract), reads=[gr, pw, ini],
                     writes=[ini])
                P.op("dve", lambda e: e.scalar_tensor_tensor(out=ini[:, 1:2], in0=gr[:, L - 1:L], scalar=sL, in1=ini[:, 3:4],
                                                             op0=ALU.mult, op1=ALU.add), reads=[gr, pw, ini], writes=[ini])
                tt(tA[:, :], gr[:, :], Ct[:, :], ALU.mult, "dve", [gr, Ct], [tA])
                tt(tB[:, :], gi[:, :], St[:, :], ALU.mult, "pool", [gi, St], [tB])
                tt(tC[:, :], gr[:, :], St[:, :], ALU.mult, "pool", [gr, St], [tC])
                tt(tD[:, :], gi[:, :], Ct[:, :], ALU.mult, "dve", [gi, Ct], [tD])
                tt(hr[:, :], tA[:, :], tB[:, :], ALU.subtract, "dve", [tA, tB], [hr])
                tt(hi[:, :], tC[:, :], tD[:, :], ALU.add, "pool", [tC, tD], [hi])
                P.op("pe", lambda e: e.matmul(py[0:32, :], lhsT=Cblk[:, 0, :], rhs=hr[:, :], start=True, stop=False),
                     reads=[Cblk, hr], writes=[py])
                P.op("pe", lambda e: e.matmul(py[0:32, :], lhsT=Cblk[:, 1, :], rhs=hi[:, :], start=False, stop=True),
                     reads=[Cblk, hi], writes=[py])
                y_ = ysb[sg % 2]
                P.op("dve", lambda e, y_=y_, usl=usl: e.scalar_tensor_tensor(out=y_[:, :], in0=usl, scalar=dsk[:, 0:1],
                                                                           in1=py[0:32, :], op0=ALU.mult, op1=ALU.add),
                     reads=[uT, dsk, py], writes=[y_])
                P.dma(un["out"][:, sg * L:(sg + 1) * L], y_[:, :], reads=[y_], writes=[obuf])
        P.barrier()


def s5_host_unit(b_, scn, u_ssm, lam_re, lam_im, log_dt, b_re, b_im, c_re, c_im, dsk):
    gs = [2 * scn, 2 * scn + 1]
    prm = np.zeros((128, 4), np.float32)
    prm[:, 0] = lam_re[gs].reshape(-1)
    prm[:, 1] = lam_im[gs].reshape(-1)
    prm[:, 2] = np.repeat(log_dt[gs], 64)
    return dict(uT=np.ascontiguousarray(u_ssm[:, 32 * scn:32 * scn + 32].T), prm=prm,
                bre=np.ascontiguousarray(b_re[gs].reshape(128, 16)), bim=np.ascontiguousarray(b_im[gs].reshape(128, 16)),
                cre=np.ascontiguousarray(c_re[gs].transpose(0, 2, 1).reshape(128, 16)),
                cim=np.ascontiguousarray(c_im[gs].transpose(0, 2, 1).reshape(128, 16)),
                dsk=np.ascontiguousarray(dsk[gs].reshape(32, 1)))


from concourse.bass_utils import run_bass_kernel_spmd
import time as _time
import sys as _sys


def _run(prog, in_maps, tag):
    t = _time.time()
    res = run_bass_kernel_spmd(prog, in_maps, core_ids=list(range(8)))
    print("[kernel] launch %s %.1fs" % (tag, _time.time() - t), file=_sys.stderr, flush=True)
    return res

NCORES = 8
BATCH = 2
SEQ = 8192
NTOK = BATCH * SEQ // NCORES
DEPTH = 4
POOL_WINDOWS = (2, 4, 8, 16)


def _di(nc, name, shape):
    return nc.dram_tensor(name, list(shape), F32, kind="ExternalInput").ap()


def _do(nc, name, shape):
    return nc.dram_tensor(name, list(shape), F32, kind="ExternalOutput").ap()


def _dint(nc, name, shape):
    return nc.dram_tensor(name, list(shape), F32, kind="Internal").ap()


def _ffn_ins(nc, pfx):
    return dict(w1r=_di(nc, pfx + "w1r", [11, 128, 8, 512]), w2r=_di(nc, pfx + "w2r", [128, 22, 1024]),
                g=_di(nc, pfx + "g", [128, 1024]), b=_di(nc, pfx + "b", [128, 1024]))


def _mix_ins(nc):
    return dict(zpT=_di(nc, "zpT", [2, 128, 16 + NTOK]), corr=_di(nc, "corr", [128, 2, 16]), invw=_di(nc, "invw", [128, 2]),
                pwblk=_di(nc, "pwblk", [2, 128, 128]), pscale=_di(nc, "pscale", [128, 2]),
                onsaT=_di(nc, "onsaT", [4, 128, NTOK]), yT=_di(nc, "yT", [2, 128, NTOK]), gluw=_di(nc, "gluw", [128, 2, 256]),
                glub=_di(nc, "glub", [128, 2]), woutr=_di(nc, "woutr", [128, 8, 1024]), g=_di(nc, "mg", [128, 1024]),
                b=_di(nc, "mb", [128, 1024]))


def build_token_prog(first, last):
    nc = bass.Bass("TRN2", target_bir_lowering=False)
    idd = _di(nc, "ident", [128, 128])
    with ExitStack() as es:
        P = Prog(nc, es)
        ident = P.sb("ident", [128, 128], F32)
        P.dma(ident[:, :], idd, writes=[ident])
        if first:
            cur = _di(nc, "h_in", [NTOK, 1024])
            curb = None
        else:
            h1 = _di(nc, "h1_in", [NTOK, 1024])
            md = _mix_ins(nc)
            h2 = _dint(nc, "h2_int", [NTOK, 1024])
            h2b = Buf("h2")
            mixout_phase(P, NTOK, h1, md, md["g"], md["b"], h2, ident, None, h2b)
            f2 = _ffn_ins(nc, "f2_")
            if last:
                h3 = _do(nc, "h_out", [NTOK, 1024])
            else:
                h3 = _dint(nc, "h3_int", [NTOK, 1024])
            h3b = Buf("h3")
            ffn_phase(P, h2, f2["w1r"], f2["w2r"], f2["g"], f2["b"], h3, NTOK, ident, h2b, h3b)
            cur, curb = h3, h3b
        if not last:
            f1 = _ffn_ins(nc, "f1_")
            h1o = _do(nc, "h1_out", [NTOK, 1024])
            h1ob = Buf("h1o")
            ffn_phase(P, cur, f1["w1r"], f1["w2r"], f1["g"], f1["b"], h1o, NTOK, ident, curb, h1ob)
            winr = _di(nc, "winr", [128, 8, NZ])
            cs = _di(nc, "cs", [NTOK, 64])
            z = _do(nc, "z_out", [NTOK, NZ])
            proj_phase(P, h1o, winr, cs, z, NTOK, ident, h1ob, Buf("z"))
        P.emit()
    return nc


def build_mixer_prog():
    S = SEQ
    NT = S // 128
    NCCH = 4
    nc = bass.Bass("TRN2", target_bir_lowering=False)
    idd = _di(nc, "ident", [128, 128])
    d = dict(qT=_di(nc, "qT", [NT, 64, 4, 128]), kcT_in=_di(nc, "kcT_in", [64, S]), vcT_in=_di(nc, "vcT_in", [64, S]),
             ksT=_di(nc, "ksT", [64, S]), kwT=_di(nc, "kwT", [64, S]), vs=_di(nc, "vs", [128, NT, 64]),
             vw=_di(nc, "vw", [128, NT, 64]), gates=_di(nc, "gates", [S, 12]), w1k=_di(nc, "w1k", [64, 32, 128]),
             w2k=_di(nc, "w2k", [128, 64]), pek=_di(nc, "pek", [64, 32]), w1v=_di(nc, "w1v", [64, 32, 128]),
             w2v=_di(nc, "w2v", [128, 64]), pev=_di(nc, "pev", [64, 32]), bias=_di(nc, "bias", [NT, 128, 128]),
             ov=_di(nc, "ov", [128, NCCH, 128]))
    o = _do(nc, "o_nsa", [S, 128])
    units = []
    for u in range(2):
        units.append(dict(uT=_di(nc, "uT%d" % u, [32, S]), prm=_di(nc, "prm%d" % u, [128, 4]),
                          bre=_di(nc, "bre%d" % u, [128, 16]), bim=_di(nc, "bim%d" % u, [128, 16]),
                          cre=_di(nc, "cre%d" % u, [128, 16]), cim=_di(nc, "cim%d" % u, [128, 16]),
                          dsk=_di(nc, "dsk%d" % u, [32, 1]), out=_do(nc, "y%d" % u, [32, S])))
    with ExitStack() as es:
        P = Prog(nc, es)
        ident = P.sb("ident", [128, 128], F32)
        P.dma(ident[:, :], idd, writes=[ident])
        s5_phase(P, S, units, ident, Buf("y"))
        nsa_phase(P, S, list(range(NT)), d, ident, o, Buf("o"))
        P.emit()
    return nc


def _rope_cs():
    inv = (1.0 / (np.float32(10000.0) ** (np.arange(0, 64, 2, dtype=np.float32) / np.float32(64)))).astype(np.float32)
    ang = (np.arange(SEQ, dtype=np.float32)[:, None] * inv[None, :]).astype(np.float32)
    return np.ascontiguousarray(np.concatenate([np.cos(ang), np.sin(ang)], axis=1).astype(np.float32))


def _ffn_host(pfx, w_in, w_out, g, b):
    return {pfx + "w1r": prep_w1(w_in), pfx + "w2r": prep_w2(w_out), pfx + "g": bc128(g), pfx + "b": bc128(b)}


_CACHE = {}
_DBG = {}


def _prog(key, fn):
    if key not in _CACHE:
        _CACHE[key] = fn()
    return _CACHE[key]


def kernel(**inp):
    f32 = lambda a: np.ascontiguousarray(np.asarray(a, dtype=np.float32))
    x = f32(inp["x"]).reshape(BATCH * SEQ, 1024)
    ident = np.eye(128, dtype=np.float32)
    cs_full = _rope_cs()
    bias_c, ov_c = nsa_consts(SEQ)
    wch = np.array([POOL_WINDOWS[(c // 64)] for c in range(256)], np.float32)
    invw = np.ascontiguousarray((1.0 / wch).reshape(2, 128).T)
    t16 = np.arange(16, dtype=np.float32)
    corr_start = (wch[:, None] / np.minimum(t16[None, :] + 1.0, wch[:, None])).astype(np.float32)
    corr_start = np.ascontiguousarray(corr_start.reshape(2, 128, 16).transpose(1, 0, 2))
    corr_one = np.ones_like(corr_start)

    h1 = None
    z = None
    out = None
    cur = x
    for l in range(DEPTH + 1):
        first = (l == 0)
        last = (l == DEPTH)
        common = {"ident": ident}
        if not first:
            lm = l - 1
            zb = z.reshape(BATCH, SEQ, NZ)
            progB = _prog("B", build_mixer_prog)
            w1k = np.ascontiguousarray(f32(inp["cmp_k_w1"][lm]).reshape(32, 64, 128).transpose(1, 0, 2))
            w1v = np.ascontiguousarray(f32(inp["cmp_v_w1"][lm]).reshape(32, 64, 128).transpose(1, 0, 2))
            w2k = f32(inp["cmp_k_w2"][lm])
            w2v = f32(inp["cmp_v_w2"][lm])
            pek = np.ascontiguousarray(f32(inp["cmp_pe_k"][lm]).T)
            pev = np.ascontiguousarray(f32(inp["cmp_pe_v"][lm]).T)
            ssmw = [f32(inp["ssm_" + k][lm]) for k in ("lam_re", "lam_im", "log_dt", "b_re", "b_im", "c_re", "c_im", "d")]
            tok = lambda v: np.ascontiguousarray(v.reshape(SEQ // 128, 128, 64).transpose(1, 0, 2))
            in_maps = []
            for c in range(NCORES):
                b_, g_, rp = c // 4, (c // 2) % 2, c % 2
                hp = [4 * g_ + (2 * rp + k) % 4 for k in range(4)]
                zz = zb[b_]
                q = np.stack([zz[:, 256 + 64 * h:256 + 64 * h + 64] for h in hp], 0)
                kvs = [zz[:, 768 + 128 * s + 64 * g_:768 + 128 * s + 64 * g_ + 64] for s in range(6)]
                gts = np.concatenate([zz[:, 1536 + 3 * h:1536 + 3 * h + 3] for h in hp], 1)
                m = dict(common)
                m.update(qT=np.ascontiguousarray(q.reshape(4, SEQ // 128, 128, 64).transpose(1, 3, 0, 2)),
                         kcT_in=np.ascontiguousarray(kvs[0].T), vcT_in=np.ascontiguousarray(kvs[1].T),
                         ksT=np.ascontiguousarray(kvs[2].T), kwT=np.ascontiguousarray(kvs[4].T), vs=tok(kvs[3]),
                         vw=tok(kvs[5]), gates=np.ascontiguousarray(gts), w1k=w1k, w2k=w2k, pek=pek, w1v=w1v, w2v=w2v,
                         pev=pev, bias=bias_c, ov=ov_c)
                for uu in range(2):
                    unit = c * 2 + uu
                    ub, scn = unit // 8, unit % 8
                    hu = s5_host_unit(ub, scn, zb[ub][:, 1560:1816], *ssmw)
                    for k, v in hu.items():
                        m["%s%d" % (k, uu)] = v
                in_maps.append(m)
            res = _run(progB, in_maps, 'B%d' % lm)
            o_nsa = np.zeros((BATCH, SEQ, 512), np.float32)
            y_ssm = np.zeros((BATCH, SEQ, 256), np.float32)
            for c in range(NCORES):
                b_, g_, rp = c // 4, (c // 2) % 2, c % 2
                o_nsa[b_, :, 256 * g_ + 128 * rp:256 * g_ + 128 * rp + 128] = res.results[c]["o_nsa"]
                for uu in range(2):
                    unit = c * 2 + uu
                    ub, scn = unit // 8, unit % 8
                    y_ssm[ub, :, 32 * scn:32 * scn + 32] = res.results[c]["y%d" % uu].T
            _DBG['o_nsa_%d' % lm] = o_nsa
            _DBG['y_ssm_%d' % lm] = y_ssm
            pw = f32(inp["pool_w"][lm])
            pwblk = np.zeros((2, 128, 128), np.float32)
            for ch in range(2):
                for k in range(2):
                    pwblk[ch, 64 * k:64 * k + 64, 64 * k:64 * k + 64] = pw[2 * ch + k]
            common.update(invw=invw, pwblk=pwblk, pscale=np.ascontiguousarray(f32(inp["pool_scale"][lm]).reshape(2, 128).T),
                          gluw=np.ascontiguousarray(f32(inp["ssm_glu_w"][lm]).reshape(2, 128, 256).transpose(1, 0, 2)),
                          glub=np.ascontiguousarray(f32(inp["ssm_glu_b"][lm]).reshape(2, 128).T),
                          woutr=np.ascontiguousarray(f32(inp["w_out"][lm]).reshape(8, 128, 1024).transpose(1, 0, 2)),
                          mg=bc128(f32(inp["ln_g"][lm, 1])), mb=bc128(f32(inp["ln_b"][lm, 1])))
            common.update(_ffn_host("f2_", f32(inp["ffn2_in"][lm]), f32(inp["ffn2_out"][lm]), f32(inp["ln_g"][lm, 2]),
                                    f32(inp["ln_b"][lm, 2])))
        if not last:
            common.update(_ffn_host("f1_", f32(inp["ffn1_in"][l]), f32(inp["ffn1_out"][l]), f32(inp["ln_g"][l, 0]),
                                    f32(inp["ln_b"][l, 0])))
            common["winr"] = prep_win(f32(inp["w_in"][l]))
        in_maps = []
        for c in range(NCORES):
            b_, j = c // 4, c % 4
            t0 = j * NTOK
            m = dict(common)
            if first:
                m["h_in"] = np.ascontiguousarray(cur[c * NTOK:(c + 1) * NTOK])
            else:
                m["h1_in"] = np.ascontiguousarray(h1[c * NTOK:(c + 1) * NTOK])
                zp = np.zeros((16 + NTOK, 256), np.float32)
                lo = max(0, t0 - 16)
                zp[16 - (t0 - lo):] = zb[b_][lo:t0 + NTOK, 0:256]
                m["zpT"] = np.ascontiguousarray(zp.T.reshape(2, 128, 16 + NTOK))
                m["corr"] = corr_start if j == 0 else corr_one
                m["onsaT"] = np.ascontiguousarray(o_nsa[b_, t0:t0 + NTOK].T.reshape(4, 128, NTOK))
                m["yT"] = np.ascontiguousarray(y_ssm[b_, t0:t0 + NTOK].T.reshape(2, 128, NTOK))
            if not last:
                m["cs"] = np.ascontiguousarray(cs_full[t0:t0 + NTOK])
            in_maps.append(m)
        prog = _prog(("T", first, last), lambda: build_token_prog(first, last))
        res = _run(prog, in_maps, 'T%d' % l)
        if last:
            out = np.concatenate([res.results[c]["h_out"] for c in range(NCORES)], 0)
        else:
            h1 = np.concatenate([res.results[c]["h1_out"] for c in range(NCORES)], 0)
            z = np.concatenate([res.results[c]["z_out"] for c in range(NCORES)], 0)
            _DBG['h1_%d' % l] = h1
            _DBG['z_%d' % l] = z
    return out.reshape(BATCH, SEQ, 1024).astype(np.float32)
```
